# Optimizing a Trainium2 kernel written in Bass

```python
import jax, jax.numpy as jnp
from jax import lax
import numpy as np

D_MODEL = 2048
BATCH = 2
SEQ = 4096
DEPTH = 4

PLE_DIM = 256
SC_WIDTH = D_MODEL // 2
SC_KERNEL = 3
CF_WIDTH = D_MODEL // 2
CF_KERNEL = 31
GLA_HEADS = 4
GLA_KEY_DIM = D_MODEL // 2
GLA_VAL_DIM = D_MODEL
GLA_DK = GLA_KEY_DIM // GLA_HEADS
GLA_DV = GLA_VAL_DIM // GLA_HEADS
GLA_GATE_RANK = 16
GLA_GATE_NORMALIZER = 16.0
GLA_CHUNK = 64
D_FF = -(-8 * D_MODEL // (3 * 256)) * 256
EPS = 1e-6

IN_SPLITS = (SC_WIDTH, SC_WIDTH, SC_WIDTH,
             CF_WIDTH, CF_WIDTH,
             GLA_KEY_DIM, GLA_KEY_DIM, GLA_VAL_DIM,
             GLA_VAL_DIM, GLA_GATE_RANK,
             D_MODEL, D_MODEL, D_MODEL)
IN_COLS = sum(IN_SPLITS)

kernel_name = "hybrid_conv_conformer_gla_trunk"


def rms_norm(x, g):
    xf = x.astype(jnp.float32)
    y = xf * lax.rsqrt(jnp.mean(xf * xf, axis=-1, keepdims=True) + EPS)
    return (y * g.astype(jnp.float32)).astype(x.dtype)


def layer_norm(x, g, b):
    xf = x.astype(jnp.float32)
    mu = jnp.mean(xf, axis=-1, keepdims=True)
    var = jnp.mean(jnp.square(xf - mu), axis=-1, keepdims=True)
    y = (xf - mu) * lax.rsqrt(var + EPS)
    return (y * g.astype(jnp.float32) + b.astype(jnp.float32)).astype(x.dtype)


def causal_depthwise_conv(u, w, b):
    k_width, channels = w.shape
    out = lax.conv_general_dilated(
        u, w[:, None, :].astype(u.dtype), window_strides=(1,), padding=[(k_width - 1, 0)],
        dimension_numbers=("NWC", "WIO", "NWC"), feature_group_count=channels)
    return out + b


def short_conv_mixer(gate_b, gate_c, xs, conv_w, conv_b, w_out):
    u = causal_depthwise_conv(gate_c * xs, conv_w, conv_b)
    return (gate_b * u) @ w_out


def conformer_conv_mixer(a, gate, conv_w, conv_b, ln_g, ln_b, w_out):
    u = a * jax.nn.sigmoid(gate)
    u = causal_depthwise_conv(u, conv_w, conv_b)
    u = layer_norm(u, ln_g, ln_b)
    return jax.nn.silu(u) @ w_out


def gla_chunked(q, k, v, gk):
    bsz, t_len, n_heads, dk = q.shape
    dv = v.shape[-1]
    n_chunks = t_len // GLA_CHUNK

    def to_chunks(a):
        return a.reshape(bsz, n_chunks, GLA_CHUNK, n_heads, a.shape[-1]).transpose(1, 0, 3, 2, 4)

    qc, kc, vc, gc = (to_chunks(a) for a in (q * (dk ** -0.5), k, v, gk))
    causal = jnp.tril(jnp.ones((GLA_CHUNK, GLA_CHUNK), dtype=bool))[:, :, None]

    def step(state, inp):
        qi, ki, vi, gi = inp
        b = jnp.cumsum(gi, axis=2)
        o_inter = jnp.einsum("bhcd,bhde->bhce", qi * jnp.exp(b), state)
        rel = b[:, :, :, None, :] - b[:, :, None, :, :]
        decay = jnp.exp(jnp.where(causal, rel, -jnp.inf))
        scores = jnp.einsum("bhid,bhjd,bhijd->bhij", qi, ki, decay)
        o = o_inter + jnp.einsum("bhij,bhje->bhie", scores, vi)
        b_last = b[:, :, -1:, :]
        state = (jnp.exp(b_last[:, :, 0, :])[..., None] * state
                 + jnp.einsum("bhcd,bhce->bhde", ki * jnp.exp(b_last - b), vi))
        return state, o

    s0 = jnp.zeros((bsz, n_heads, dk, dv), jnp.float32)
    _, o = lax.scan(step, s0, (qc, kc, vc, gc))
    return o.transpose(1, 0, 3, 2, 4).reshape(bsz, t_len, n_heads, dv)


def gla_mixer(q, k, v, g_out, gk_lr, w_gk, b_gk, g_norm, w_out):
    bsz, t_len, _ = q.shape
    gk = jax.nn.log_sigmoid((gk_lr @ w_gk + b_gk).astype(jnp.float32)) / GLA_GATE_NORMALIZER
    split = lambda a, d: a.astype(jnp.float32).reshape(bsz, t_len, GLA_HEADS, d)
    o = gla_chunked(split(q, GLA_DK), split(k, GLA_DK), split(v, GLA_DV), split(gk, GLA_DK))
    o = o * lax.rsqrt(jnp.mean(o * o, axis=-1, keepdims=True) + EPS) * g_norm.astype(jnp.float32)
    o = o.reshape(bsz, t_len, GLA_VAL_DIM).astype(q.dtype) * jax.nn.silu(g_out)
    return o @ w_out


def setup_inputs(seed: int = 0) -> dict:
    key = jax.random.key(seed)
    ks = jax.random.split(key, 24)
    nrm = lambda k, shape, scale: jax.random.normal(k, shape, jnp.float32) * scale
    gain = lambda k, shape: 1.0 + 0.05 * jax.random.normal(k, shape, jnp.float32)
    L = DEPTH
    return {
        "x": nrm(ks[0], (BATCH, SEQ, D_MODEL), 1.0),
        "p": nrm(ks[1], (DEPTH, BATCH, SEQ, PLE_DIM), 1.0),
        "g_mix": gain(ks[2], (L, D_MODEL)),
        "w_in": nrm(ks[3], (L, D_MODEL, IN_COLS), D_MODEL ** -0.5),
        "sc_conv_w": nrm(ks[4], (L, SC_KERNEL, SC_WIDTH), SC_KERNEL ** -0.5),
        "sc_conv_b": nrm(ks[5], (L, SC_WIDTH), 0.01),
        "w_sc_out": nrm(ks[6], (L, SC_WIDTH, D_MODEL), SC_WIDTH ** -0.5),
        "cf_conv_w": nrm(ks[7], (L, CF_KERNEL, CF_WIDTH), CF_KERNEL ** -0.5),
        "cf_conv_b": nrm(ks[8], (L, CF_WIDTH), 0.01),
        "cf_ln_g": gain(ks[9], (L, CF_WIDTH)),
        "cf_ln_b": nrm(ks[10], (L, CF_WIDTH), 0.01),
        "w_cf_out": nrm(ks[11], (L, CF_WIDTH, D_MODEL), CF_WIDTH ** -0.5),
        "w_gla_gk": nrm(ks[12], (L, GLA_GATE_RANK, GLA_KEY_DIM), GLA_GATE_RANK ** -0.5),
        "b_gla_gk": nrm(ks[13], (L, GLA_KEY_DIM), 0.01),
        "g_gla_norm": gain(ks[14], (L, GLA_DV)),
        "w_gla_out": nrm(ks[15], (L, GLA_VAL_DIM, D_MODEL), GLA_VAL_DIM ** -0.5),
        "w_o": nrm(ks[16], (L, D_MODEL, D_MODEL), D_MODEL ** -0.5),
        "g_ffn": gain(ks[17], (L, D_MODEL)),
        "w_gate_up": nrm(ks[18], (L, D_MODEL, 2 * D_FF), D_MODEL ** -0.5),
        "w_down": nrm(ks[19], (L, D_FF, D_MODEL), D_FF ** -0.5),
        "g_ple": gain(ks[20], (L, D_MODEL)),
        "w_ple_gate": nrm(ks[21], (L, D_MODEL, D_MODEL), D_MODEL ** -0.5),
        "w_ple": nrm(ks[22], (L, PLE_DIM, D_MODEL), PLE_DIM ** -0.5),
        "g_final": gain(ks[23], (D_MODEL,)),
    }


def reference(x, p, g_mix, w_in, sc_conv_w, sc_conv_b, w_sc_out, cf_conv_w, cf_conv_b,
              cf_ln_g, cf_ln_b, w_cf_out, w_gla_gk, b_gla_gk, g_gla_norm, w_gla_out, w_o,
              g_ffn, w_gate_up, w_down, g_ple, w_ple_gate, w_ple, g_final):
    split_points = np.cumsum(IN_SPLITS)[:-1].tolist()
    for i in range(DEPTH):
        h = rms_norm(x, g_mix[i])
        z = h @ w_in[i]
        (sc_b, sc_c, sc_x, cf_a, cf_g, q, k, v, g_out, gk_lr,
         m_a, m_b, m_c) = jnp.split(z, split_points, axis=-1)
        u_a = short_conv_mixer(sc_b, sc_c, sc_x, sc_conv_w[i], sc_conv_b[i], w_sc_out[i])
        u_b = conformer_conv_mixer(cf_a, cf_g, cf_conv_w[i], cf_conv_b[i],
                                   cf_ln_g[i], cf_ln_b[i], w_cf_out[i])
        u_c = gla_mixer(q, k, v, g_out, gk_lr, w_gla_gk[i], b_gla_gk[i],
                        g_gla_norm[i], w_gla_out[i])
        merged = (jax.nn.sigmoid(m_a) * u_a + jax.nn.sigmoid(m_b) * u_b
                  + jax.nn.sigmoid(m_c) * u_c)
        x = x + merged @ w_o[i]
        h = rms_norm(x, g_ffn[i])
        gate, up = jnp.split(h @ w_gate_up[i], 2, axis=-1)
        x = x + (jax.nn.silu(gate) * up) @ w_down[i]
        x = x + (p[i] @ w_ple[i]) * jax.nn.sigmoid(rms_norm(x, g_ple[i]) @ w_ple_gate[i])
    return rms_norm(x, g_final)
```

```python
import numpy as np
import concourse.bass as bass
import concourse.mybir as mybir
from concourse.bass_utils import run_bass_kernel_spmd
from contextlib import ExitStack

F32 = mybir.dt.float32
BF16 = mybir.dt.bfloat16
AF = mybir.ActivationFunctionType
ALU = mybir.AluOpType

NCORES = 8
DEPTH = 4
D = 2048
NCH = 16
T = 1024
TH = 512
HALO = 32
NH = 4
DK = 256
DV = 512
DFF = 5632
NFG = 4
FGS = 11
PLE = 256
EPS = 1e-6
IN_COLS = 17424
C_SCB, C_SCC, C_SCX, C_CFA, C_CFG = 0, 1024, 2048, 3072, 4096
C_Q, C_K, C_V, C_GO, C_GK = 5120, 6144, 7168, 9216, 11264
C_MA, C_MB, C_MC = 11280, 13328, 15376
NSLOT = 6

V_GMIX, V_GFFN, V_GPLE = 0, 16, 32
V_SCW = 48
V_SCB = 72
V_CFW = 80
V_CFB = 328
V_LNG = 336
V_LNB = 344
V_GNORM = 352
V_GFIN = 356
NV = 372


class Buf:
    __slots__ = ("name", "lw", "rd")

    def __init__(self, name):
        self.name = name
        self.lw = None
        self.rd = {}


class Eng:
    def __init__(self, name, sem):
        self.name = name
        self.sem = sem
        self.count = 0
        self.seen = {}
        self.ops = []


class Prog:
    ENGS = ("pe", "act", "dve", "pool", "sp")

    def __init__(self, nc, es):
        self.nc = nc
        self.es = es
        self.sems = {}
        self.eng = {}
        for e in self.ENGS:
            s = es.enter_context(nc.semaphore("s_" + e))
            self.sems["E" + e] = s
            self.eng[e] = Eng(e, "E" + e)
        self.dma_count = {}
        self.n_dsem = 0
        self.all_dma_toks = {}

    def new_dma_sem(self):
        k = "D%d" % self.n_dsem
        self.n_dsem += 1
        self.sems[k] = self.es.enter_context(self.nc.semaphore("d_%d" % (self.n_dsem - 1)))
        self.dma_count[k] = 0
        return k

    def _need(self, e, tok, waits):
        if tok is None:
            return
        k, v = tok
        if k == e.sem:
            if e.name in ("pe", "sp"):
                return
            if e.count - v >= 2:
                return
        if e.seen.get(k, 0) >= v:
            return
        e.seen[k] = v
        waits.append((k, v))

    def _deps(self, e, reads, writes):
        waits = []
        for b in reads:
            self._need(e, b.lw, waits)
        for b in writes:
            self._need(e, b.lw, waits)
            for k, v in b.rd.items():
                self._need(e, (k, v), waits)
        return waits

    def _mark(self, tok, reads, writes):
        k, v = tok
        for b in reads:
            if b.rd.get(k, 0) < v:
                b.rd[k] = v
        for b in writes:
            b.lw = tok
            b.rd = {}

    def op(self, eng, fn, reads=(), writes=(), inc=True):
        e = self.eng[eng]
        waits = self._deps(e, reads, writes)
        if inc:
            e.count += 1
            tok = (e.sem, e.count)
            incs = [(e.sem, 1)]
        else:
            tok = (e.sem, e.count + 1)
            incs = []
        self._mark(tok, reads, writes)
        e.ops.append((waits, fn, incs))
        return tok

    def dma(self, q, fn, dsem, reads=(), writes=(), inc=16):
        e = self.eng[q]
        waits = self._deps(e, reads, writes)
        self.dma_count[dsem] += inc
        tok = (dsem, self.dma_count[dsem])
        self._mark(tok, reads, writes)
        e.ops.append((waits, fn, [(dsem, inc)]))
        self.all_dma_toks[dsem] = tok
        return tok

    def wait_all(self, eng, toks):
        e = self.eng[eng]
        waits = []
        for t in toks:
            self._need(e, t, waits)
        e.ops.append((waits, None, []))

    def barrier(self):
        toks = [(self.eng[n].sem, self.eng[n].count) for n in ("pe", "act", "dve") if self.eng[n].count > 0]
        toks += list(self.all_dma_toks.values())
        for n in ("pe", "act", "dve", "sp"):
            e = self.eng[n]
            waits = []
            for (k, v) in toks:
                if k == e.sem:
                    continue
                if e.seen.get(k, 0) >= v:
                    continue
                e.seen[k] = v
                waits.append((k, v))
            e.ops.append((waits, None, []))

    def emit(self):
        nc = self.nc
        sems = self.sems

        def run(handle, e):
            for waits, fn, incs in e.ops:
                for k, v in waits:
                    handle.wait_ge(sems[k], v)
                if fn is None:
                    continue
                ins = fn(handle)
                for k, n in incs:
                    ins = ins.then_inc(sems[k], n)

        with nc.Block() as block:
            @block.tensor
            def _(h):
                run(h, self.eng["pe"])

            @block.scalar
            def _(h):
                run(h, self.eng["act"])

            @block.vector
            def _(h):
                run(h, self.eng["dve"])

            @block.gpsimd
            def _(h):
                run(h, self.eng["pool"])

            @block.sync
            def _(h):
                run(h, self.eng["sp"])


class KcBufs(list):
    pass


class Pool:
    def __init__(self, views):
        self.views = views
        self.bufs = [Buf("pool") for _ in views]
        self.i = 0

    def get(self):
        i = self.i % len(self.views)
        self.i += 1
        return self.views[i], self.bufs[i]


def mixer_keys():
    keys = []
    for j in range(8):
        keys += [("in", C_SCC + 128 * j), ("in", C_SCX + 128 * j), ("in", C_SCB + 128 * j)]
    for j in range(8):
        keys += [("in", C_CFG + 128 * j), ("in", C_CFA + 128 * j)]
    keys += [("in", C_GK)]
    for h in range(NH):
        keys += [("in", C_K + DK * h + 128 * i) for i in range(2)]
        keys += [("in", C_Q + DK * h + 128 * i) for i in range(2)]
        keys += [("in", C_V + DV * h + 128 * i) for i in range(4)]
        keys += [("in", C_GO + DV * h + 128 * i) for i in range(4)]
    for n in range(NCH):
        keys += [("outab", n), ("glaout", n), ("in", C_MA + 128 * n), ("in", C_MB + 128 * n), ("in", C_MC + 128 * n)]
    for n in range(NCH):
        keys += [("wo", n)]
    return keys


def main_keys():
    keys = mixer_keys()
    for fg in range(NFG):
        for f in range(FGS):
            keys += [("gate", fg * FGS + f), ("up", fg * FGS + f)]
        for n in range(NCH):
            keys += [("down", fg, n)]
    for n in range(NCH):
        keys += [("plegate", n)]
    keys += [("ple", 0), ("ple", 1)]
    return keys


def pre_keys():
    keys = [("in", C_GK)]
    for h in range(NH):
        keys += [("in", C_K + DK * h + 128 * i) for i in range(2)]
        keys += [("in", C_V + DV * h + 128 * i) for i in range(4)]
    return keys


def _tile_from(M):
    K, nc_ = M.shape
    kc = K // 128
    out = np.zeros((128, 2048), np.float32)
    v = out[:, :kc * 128].reshape(128, kc, 128)
    v[:, :, :nc_] = M.reshape(kc, 128, nc_).transpose(1, 0, 2)
    return out


def pack_tiles(w, l, keys):
    tiles = np.empty((len(keys), 128, 2048), np.float32)
    for i, key in enumerate(keys):
        kind = key[0]
        if kind == "in":
            c0 = key[1]
            ncols = 16 if c0 == C_GK else 128
            tiles[i] = _tile_from(w["w_in"][l][:, c0:c0 + ncols])
        elif kind == "outab":
            n = key[1]
            M = np.concatenate([w["w_sc_out"][l][:, n * 128:(n + 1) * 128], w["w_cf_out"][l][:, n * 128:(n + 1) * 128]], axis=0)
            tiles[i] = _tile_from(M)
        elif kind == "glaout":
            n = key[1]
            tiles[i] = _tile_from(w["w_gla_out"][l][:, n * 128:(n + 1) * 128])
        elif kind == "wo":
            n = key[1]
            tiles[i] = _tile_from(w["w_o"][l][:, n * 128:(n + 1) * 128])
        elif kind == "gate":
            f = key[1]
            tiles[i] = _tile_from(w["w_gate_up"][l][:, f * 128:(f + 1) * 128])
        elif kind == "up":
            f = key[1]
            tiles[i] = _tile_from(w["w_gate_up"][l][:, DFF + f * 128:DFF + (f + 1) * 128])
        elif kind == "down":
            fg, n = key[1], key[2]
            tiles[i] = _tile_from(w["w_down"][l][fg * FGS * 128:(fg + 1) * FGS * 128, n * 128:(n + 1) * 128])
        elif kind == "plegate":
            n = key[1]
            tiles[i] = _tile_from(w["w_ple_gate"][l][:, n * 128:(n + 1) * 128])
        elif kind == "ple":
            g = key[1]
            M = w["w_ple"][l].reshape(2, 128, 16, 128)[:, :, g * 8:(g + 1) * 8, :]
            tiles[i] = np.ascontiguousarray(M.transpose(1, 2, 0, 3)).reshape(128, 2048)
        else:
            raise ValueError(key)
    return tiles


def pack_vec(w, l):
    v = np.zeros((128, NV), np.float32)

    def fm(x):
        return np.asarray(x, np.float32).reshape(-1, 128).T

    v[:, V_GMIX:V_GMIX + 16] = fm(w["g_mix"][l])
    v[:, V_GFFN:V_GFFN + 16] = fm(w["g_ffn"][l])
    v[:, V_GPLE:V_GPLE + 16] = fm(w["g_ple"][l])
    for k in range(3):
        v[:, V_SCW + k * 8:V_SCW + k * 8 + 8] = fm(w["sc_conv_w"][l][k])
    v[:, V_SCB:V_SCB + 8] = fm(w["sc_conv_b"][l])
    for k in range(31):
        v[:, V_CFW + k * 8:V_CFW + k * 8 + 8] = fm(w["cf_conv_w"][l][k])
    v[:, V_CFB:V_CFB + 8] = fm(w["cf_conv_b"][l])
    v[:, V_LNG:V_LNG + 8] = fm(w["cf_ln_g"][l])
    v[:, V_LNB:V_LNB + 8] = fm(w["cf_ln_b"][l])
    v[:, V_GNORM:V_GNORM + 4] = fm(w["g_gla_norm"][l])
    v[:, V_GFIN:V_GFIN + 16] = fm(w["g_final"])
    return v


def pack_wgk(w, l):
    m = np.zeros((32, 1024), np.float32)
    m[0:16] = w["w_gla_gk"][l]
    m[16] = w["b_gla_gk"][l]
    return m


def make_const():
    c = np.zeros((128, 512), np.float32)
    c[:, 384:512] = np.eye(128, dtype=np.float32)
    j = np.arange(128)[:, None]
    i = np.arange(128)[None, :]
    c[:, 0:128] = (j <= i).astype(np.float32)
    c[:, 128:256] = (j <= i).astype(np.float32) * (-1.0 / 16.0)
    c[:, 256:384] = (j > i).astype(np.float32) * (-1.0 / 16.0)
    return c


class WStream:
    def __init__(self, K, dram, keys):
        self.K = K
        self.dram = dram
        self.index = {}
        for i, k in enumerate(keys):
            self.index.setdefault(k, i)

    def load(self, key, ncols=2048):
        K = self.K
        s = K.ring_i % NSLOT
        K.ring_i += 1
        idx = self.index[key]
        slot = K.ring[s]
        dram = self.dram
        K.P.dma("pool", lambda h, s=s, idx=idx, ncols=ncols: h.dma_start(out=slot[:, 0:ncols], in_=dram[idx, :, 0:ncols]),
                K.ring_sem[s], writes=[K.ring_buf[s]])
        K.ring_gen[s] += 1
        return (slot, K.ring_buf[s], s, K.ring_gen[s])


class K:
    def __init__(self, mode):
        self.mode = mode
        self._nsems = {}
        self.nc = bass.Bass("TRN2", target_bir_lowering=False)

    def dram_in(self, name, shape):
        return self.nc.dram_tensor(name, list(shape), F32, kind="ExternalInput").ap()

    def dram_out(self, name, shape):
        return self.nc.dram_tensor(name, list(shape), F32, kind="ExternalOutput").ap()

    def sb(self, name, shape, dtype):
        return self.es.enter_context(self.nc.sbuf_tensor(name, list(shape), dtype))

    def arena_f32(self, nwords):
        off = self.aoff
        self.aoff += nwords
        assert self.aoff <= self.asize, (self.aoff, self.asize)
        return self.arena[:, off:off + nwords]

    def arena_bf16(self, nelem):
        assert nelem % 2 == 0
        return self.arena_f32(nelem // 2).bitcast(BF16)

    def act(self, out, in_, func, reads, writes, **kw):
        return self.P.op("act", lambda h: h.activation(out=out, in_=in_, func=func, **kw), reads, writes)

    def tt(self, out, in0, in1, op, reads, writes, eng="dve"):
        return self.P.op(eng, lambda h: h.tensor_tensor(out=out, in0=in0, in1=in1, op=op), reads, writes)

    def ts(self, out, in0, s1, s2, op0, op1, reads, writes, eng="dve"):
        if s2 is None:
            return self.P.op(eng, lambda h: h.tensor_scalar(out=out, in0=in0, scalar1=s1, scalar2=None, op0=op0), reads, writes)
        return self.P.op(eng, lambda h: h.tensor_scalar(out=out, in0=in0, scalar1=s1, scalar2=s2, op0=op0, op1=op1), reads, writes)

    def stt(self, out, in0, scalar, in1, op0, op1, reads, writes, eng="dve"):
        return self.P.op(eng, lambda h: h.scalar_tensor_tensor(out=out, in0=in0, scalar=scalar, in1=in1, op0=op0, op1=op1),
                         reads, writes)

    def mm_group(self, ps_ap, psbuf, pairs):
        n = len(pairs)
        for i, (l, r, bufs) in enumerate(pairs):
            self.P.op("pe", lambda h, l=l, r=r, i=i: h.matmul(ps_ap, lhsT=l, rhs=r, start=(i == 0), stop=(i == n - 1)),
                      reads=bufs, writes=[psbuf], inc=(i == n - 1))

    def ps(self, pool):
        return self.pspool[pool].get()

    def wuse(self, wt):
        slot, buf, s, gen = wt
        assert self.ring_gen[s] == gen, "weight tile used after its ring slot was reloaded"
        return slot, buf

    def w3(self, wt, kc=16):
        slot, buf = self.wuse(wt)
        return slot[:, 0:kc * 128].rearrange("p (k j) -> p k j", j=128), buf

    def rmsnorm(self, xs, xbufs, gcol, outs, outbufs, ncols, vec, ones=None, out_f32=False):
        ones = self.ones_d if ones is None else ones
        nchunk = len(xs)
        pst, psb = self.ps("ax")
        for c in range(nchunk):
            sq, sqb = self.tb.get()
            self.act(sq[:, 0:ncols], xs[c], AF.Square, [xbufs[c]], [sqb])
            self.P.op("pe", lambda h, c=c, sq=sq: h.matmul(pst[:, 0:ncols], lhsT=ones[:], rhs=sq[:, 0:ncols],
                                                           start=(c == 0), stop=(c == nchunk - 1)),
                      reads=[sqb, self.cbuf], writes=[psb])
        rs, rsb = self.tf.get()
        self.act(rs[:, 0:ncols], pst[:, 0:ncols], AF.Ln, [psb], [rsb], bias=self.eps_ap[:, 0:1])
        self.act(rs[:, 0:ncols], rs[:, 0:ncols], AF.Exp, [rsb], [rsb], scale=-0.5)
        for c in range(nchunk):
            self.stt(outs[c], xs[c], vec[:, gcol + c:gcol + c + 1], rs[:, 0:ncols], ALU.mult, ALU.mult,
                     [xbufs[c], rsb, self.vbuf], [outbufs[c]])

    def proj(self, wt, rhs_of_kc, rhs_bufs, ps_ap, psbuf, kc0=0, nkc=16, lcols=slice(0, 128)):
        w3, wb = self.w3(wt)
        if isinstance(rhs_bufs, KcBufs):
            pairs = [(w3[:, kc0 + kc, lcols], rhs_of_kc(kc), [wb, rhs_bufs[kc]]) for kc in range(nkc)]
        else:
            pairs = [(w3[:, kc0 + kc, lcols], rhs_of_kc(kc), [wb] + list(rhs_bufs)) for kc in range(nkc)]
        self.mm_group(ps_ap, psbuf, pairs)

    def gla_head(self, W, h, hf, H, Hb, vec, state_only):
        P = self.P
        S = self.S
        Sbuf = self.Sbuf
        if (not state_only) and self.use_stash:
            self.gla_head_prep_stash(W, h, hf, H, Hb)
        else:
            self.gla_head_prep_full(W, h, hf, H, Hb, state_only)
        self.gla_head_chunks(h, hf, vec, state_only)

    def gla_head_prep_stash(self, W, h, hf, H, Hb):
        P = self.P
        S, Sbuf = self.S, self.Sbuf
        idx = h * 2 + hf
        sb = self.stbuf[idx]
        P.dma("sp", lambda hh: hh.dma_start(out=self.EB, in_=self.st_eb[idx].rearrange("p (i t) -> p i t", t=TH)), self.nsem("ld_eb"),
              reads=[sb[2]], writes=list(self.ebbuf) + self.al_eb)
        P.dma("sp", lambda hh: hh.dma_start(out=self.KDEC, in_=self.st_kd[idx].rearrange("p (t d) -> p t d", d=DK)), self.nsem("ld_kd"),
              reads=[sb[0]], writes=self.kdbuf)
        P.dma("sp", lambda hh: hh.dma_start(out=self.VT, in_=self.st_vt[idx].rearrange("p (t d) -> p t d", d=DV)), self.nsem("ld_vt"),
              reads=[sb[1]], writes=self.vtbuf)
        wk = [W.load(("in", C_K + DK * h + 128 * i)) for i in range(2)]
        for i in range(2):
            enb, enbb = self.tf.get()
            P.dma("sp", lambda hh, i=i, enb=enb: hh.dma_start(out=enb[:, :], in_=self.st_enb[idx, i]), self.nsem("ld_enb%d" % i),
                  reads=[sb[3 + i]], writes=[enbb])
            pk, pkb = self.ps("mm")
            self.proj(wk[i], lambda kc: H[:, kc, :], Hb, pk[:, :], pkb)
            self.tt(self.KK[:, i, :], pk[:, :], enb[:, :], ALU.mult, [pkb, enbb], [self.kkbuf[i], self.al_kk[i]])
        wq = [W.load(("in", C_Q + DK * h + 128 * i)) for i in range(2)]
        for i in range(2):
            pq, pqb = self.ps("mm")
            self.proj(wq[i], lambda kc: H[:, kc, :], Hb, pq[:, :], pqb)
            self.stt(self.QQ[:, i, :], pq[:, :], 1.0 / 16.0, self.EB[:, i, :], ALU.mult, ALU.mult,
                     [pqb, self.ebbuf[i]], [self.qqbuf[i], self.al_qq[i]])
        wgo = [W.load(("in", C_GO + DV * h + 128 * i)) for i in range(4)]
        for i in range(4):
            pgo, pgob = self.ps("mm")
            self.proj(wgo[i], lambda kc: H[:, kc, :], Hb, pgo[:, :], pgob)
            self.act(self.SG[:, i, :], pgo[:, :], AF.Silu, [pgob], [self.sgbuf[i], self.al_sg[i]])
        for i in range(2):
            self.act(self.SB16[:, i, :], S[:, h * 2 + i, :], AF.Copy, [Sbuf[h * 2 + i]], [self.sb16buf[i]])

    def gla_head_prep_full(self, W, h, hf, H, Hb, state_only):
        P = self.P
        S = self.S
        Sbuf = self.Sbuf
        wk = [W.load(("in", C_K + DK * h + 128 * i)) for i in range(2)]
        for t in range(4):
            pg, pgb = self.ps("ax")
            P.op("pe", lambda hh, t=t, pg=pg: hh.matmul(pg[:, 0:DK], lhsT=self.GLR[:, t * 128:(t + 1) * 128],
                                                       rhs=self.WGK[:, h * DK:(h + 1) * DK], start=True, stop=True),
                 reads=[self.glrbuf, self.cbuf], writes=[pgb])
            e, eb = self.tf.get()
            self.act(e[:, 0:DK], pg[:, 0:DK], AF.Exp, [pgb], [eb], scale=-1.0)
            self.act(self.GT[:, t, :], e[:, 0:DK], AF.Ln, [eb], [self.gtbuf[t]], bias=self.one_ap[:, 0:1])
        def values_block():
            wv = [W.load(("in", C_V + DV * h + 128 * i)) for i in range(4)]
            for t in range(4):
                pv, pvb = self.ps("mm")
                for i in range(4):
                    w3, wb = self.w3(wv[i])
                    pairs = [(H[:, kc, t * 128:(t + 1) * 128], w3[:, kc, :], [wb, Hb[kc]]) for kc in range(16)]
                    self.mm_group(pv[:, i * 128:(i + 1) * 128], pvb, pairs)
                self.act(self.VT[:, t, :], pv[:, :], AF.Copy, [pvb], [self.vtbuf[t]])

        if state_only:
            values_block()
        for i in range(2):
            pb, pbb = self.ps("ax")
            for t in range(4):
                P.op("pe", lambda hh, t=t, i=i, pb=pb: hh.matmul(pb[:, t * 128:(t + 1) * 128],
                                                               lhsT=self.GT[:, t, i * 128:(i + 1) * 128],
                                                               rhs=self.CONST[:, 128:256], start=True, stop=True),
                     reads=[self.gtbuf[t], self.cbuf], writes=[pbb])
            self.act(self.EB[:, i, :], pb[:, :], AF.Exp, [pbb], [self.ebbuf[i]])
            if state_only and self.use_stash:
                enb, enbb = self.tf.get()
                self.act(enb[:, :], pb[:, :], AF.Exp, [pbb], [enbb], scale=-1.0)
                idx_ = h * 2 + hf
                P.dma("sp", lambda hh, i=i, enb=enb, idx_=idx_: hh.dma_start(out=self.st_enb[idx_, i], in_=enb[:, :]),
                      self.nsem("st_enb%d" % i), reads=[enbb], writes=[self.stbuf[idx_][3 + i]])
            if not state_only:
                enb, enbb = self.tf.get()
                self.act(enb[:, :], pb[:, :], AF.Exp, [pbb], [enbb], scale=-1.0)
                pk, pkb = self.ps("mm")
                self.proj(wk[i], lambda kc: H[:, kc, :], Hb, pk[:, :], pkb)
                self.tt(self.KK[:, i, :], pk[:, :], enb[:, :], ALU.mult, [pkb, enbb], [self.kkbuf[i]])
        for t in range(4):
            pd, pdb = self.ps("ax")
            P.op("pe", lambda hh, t=t, pd=pd: hh.matmul(pd[:, 0:DK], lhsT=self.CONST[:, 256:384], rhs=self.GT[:, t, :],
                                                       start=True, stop=True),
                 reads=[self.gtbuf[t], self.cbuf], writes=[pdb])
            ed, edb = self.tf.get()
            self.act(ed[:, 0:DK], pd[:, 0:DK], AF.Exp, [pdb], [edb])
            pkt, pktb = self.ps("mm")
            for i in range(2):
                w3, wb = self.w3(wk[i])
                pairs = [(H[:, kc, t * 128:(t + 1) * 128], w3[:, kc, :], [wb, Hb[kc]]) for kc in range(16)]
                self.mm_group(pkt[:, i * 128:(i + 1) * 128], pktb, pairs)
            self.tt(self.KDEC[:, t, :], pkt[:, 0:DK], ed[:, 0:DK], ALU.mult, [pktb, edb], [self.kdbuf[t]])
        if not state_only:
            wq = [W.load(("in", C_Q + DK * h + 128 * i)) for i in range(2)]
            for i in range(2):
                pq, pqb = self.ps("mm")
                self.proj(wq[i], lambda kc: H[:, kc, :], Hb, pq[:, :], pqb)
                self.stt(self.QQ[:, i, :], pq[:, :], 1.0 / 16.0, self.EB[:, i, :], ALU.mult, ALU.mult,
                         [pqb, self.ebbuf[i]], [self.qqbuf[i]])
        if not state_only:
            values_block()
        if not state_only:
            wgo = [W.load(("in", C_GO + DV * h + 128 * i)) for i in range(4)]
            for i in range(4):
                pgo, pgob = self.ps("mm")
                self.proj(wgo[i], lambda kc: H[:, kc, :], Hb, pgo[:, :], pgob)
                self.act(self.SG[:, i, :], pgo[:, :], AF.Silu, [pgob], [self.sgbuf[i]])
            for i in range(2):
                self.act(self.SB16[:, i, :], S[:, h * 2 + i, :], AF.Copy, [Sbuf[h * 2 + i]], [self.sb16buf[i]])
        if state_only and self.use_stash:
            idx = h * 2 + hf
            sb = self.stbuf[idx]
            P.dma("sp", lambda hh: hh.dma_start(out=self.st_kd[idx].rearrange("p (t d) -> p t d", d=DK), in_=self.KDEC), self.nsem("st_kd"),
                  reads=self.kdbuf, writes=[sb[0]])
            P.dma("sp", lambda hh: hh.dma_start(out=self.st_vt[idx].rearrange("p (t d) -> p t d", d=DV), in_=self.VT), self.nsem("st_vt"),
                  reads=self.vtbuf, writes=[sb[1]])
            P.dma("sp", lambda hh: hh.dma_start(out=self.st_eb[idx].rearrange("p (i t) -> p i t", t=TH), in_=self.EB), self.nsem("st_eb"),
                  reads=self.ebbuf, writes=[sb[2]])

    def gla_head_chunks(self, h, hf, vec, state_only):
        P = self.P
        S = self.S
        Sbuf = self.Sbuf
        for t in range(4):
            tc_ = slice(t * 128, (t + 1) * 128)
            lastc = t * 128 + 127
            if not state_only:
                pss, pssb = self.ps("gl")
                pairs = [(self.KK[:, i, tc_], self.QQ[:, i, tc_], [self.kkbuf[i], self.qqbuf[i]]) for i in range(2)]
                self.mm_group(pss[:, 0:128], pssb, pairs)
                sct, sctb = self.tb.get()
                self.tt(sct[:, 0:128], pss[:, 0:128], self.CONST[:, 0:128], ALU.mult, [pssb, self.cbuf], [sctb])
            pds_l = []
            for i in range(2):
                pds, pdsb = self.ps("mm")
                self.mm_group(pds[:, :], pdsb, [(self.KDEC[:, t, i * 128:(i + 1) * 128], self.VT[:, t, :],
                                                 [self.kdbuf[t], self.vtbuf[t]])])
                pds_l.append((pds, pdsb))
            if not state_only:
                po, pob = self.ps("gl")
                for e in range(4):
                    ec = slice(e * 128, (e + 1) * 128)
                    pairs = [(self.SB16[:, i, ec], self.QQ[:, i, tc_], [self.sb16buf[i], self.qqbuf[i]]) for i in range(2)]
                    pairs.append((self.VT[:, t, ec], sct[:, 0:128], [self.vtbuf[t], sctb]))
                    self.mm_group(po[:, ec], pob, pairs)
                self.act(self.O[:, :, tc_], po[:, :].rearrange("p (e t) -> p e t", t=128), AF.Copy, [pob], [self.obuf] + (self.al_o if self.use_stash else []))
            for i in range(2):
                pds, pdsb = pds_l[i]
                self.stt(S[:, h * 2 + i, :], S[:, h * 2 + i, :], self.EB[:, i, lastc:lastc + 1], pds[:, :],
                         ALU.mult, ALU.add, [Sbuf[h * 2 + i], self.ebbuf[i], pdsb], [Sbuf[h * 2 + i]])
                if state_only:
                    self.tt(self.DACC[:, h * 2 + i:h * 2 + i + 1], self.DACC[:, h * 2 + i:h * 2 + i + 1],
                            self.EB[:, i, lastc:lastc + 1], ALU.mult, [self.daccbuf, self.ebbuf[i]], [self.daccbuf])
                elif t < 3:
                    self.act(self.SB16[:, i, :], S[:, h * 2 + i, :], AF.Copy, [Sbuf[h * 2 + i]], [self.sb16buf[i]])
        if state_only:
            return
        pn, pnb = self.ps("ax")
        for e in range(4):
            sq, sqb = self.tb.get()
            self.act(sq[:, :], self.O[:, e, :], AF.Square, [self.obuf], [sqb])
            P.op("pe", lambda hh, e=e, sq=sq: hh.matmul(pn[:, :], lhsT=self.ones_dv[:], rhs=sq[:, :], start=(e == 0), stop=(e == 3)),
                 reads=[sqb, self.cbuf], writes=[pnb])
        rs = self.RS
        rsbufs = [self.gtbuf[0], self.gtbuf[1]]
        self.act(rs, pn[:, :], AF.Ln, [pnb], rsbufs + (self.al_rs if self.use_stash else []), bias=self.eps_ap[:, 0:1])
        self.act(rs, rs, AF.Exp, rsbufs, rsbufs, scale=-0.5)
        for e in range(4):
            t1, t1b = self.tf.get()
            self.tt(t1[:, :], self.O[:, e, :], rs, ALU.mult, [self.obuf] + rsbufs, [t1b])
            self.stt(self.G[:, h * 4 + e, :], t1[:, :], vec[:, V_GNORM + e:V_GNORM + e + 1], self.SG[:, e, :],
                     ALU.mult, ALU.mult, [t1b, self.sgbuf[e], self.vbuf],
                     [self.gbuf] + ([self.ucfbuf[(h * 4 + e) // 2]] if self.use_stash else []))

    def glr_proj(self, W, H, Hb):
        wg = W.load(("in", C_GK))
        pg, pgb = self.ps("ax")
        self.proj(wg, lambda kc: H[:, kc, :], Hb, pg[0:16, :], pgb, lcols=slice(0, 16))
        self.act(self.GLR[0:16, :], pg[0:16, :], AF.Copy, [pgb], [self.glrbuf])

    def mixer_half(self, W, hf, vec, state_cb=None):
        P = self.P
        X, XB = self.X, self.XB
        H, Hb = self.H, self.Hbuf
        c0 = hf * TH
        cs = slice(c0, c0 + TH)
        self.rmsnorm([X[:, c, cs] for c in range(NCH)], [XB[c][hf] for c in range(NCH)], V_GMIX,
                     [H[:, c, :] for c in range(NCH)], Hb, TH, vec)
        for j in range(8):
            wc = W.load(("in", C_SCC + 128 * j))
            wx = W.load(("in", C_SCX + 128 * j))
            wb_ = W.load(("in", C_SCB + 128 * j))
            ub, ubb = self.ubp.get()
            pc, pcb = self.ps("mm")
            self.proj(wc, lambda kc: H[:, kc, :], Hb, pc[:, :], pcb)
            px, pxb = self.ps("mm")
            self.proj(wx, lambda kc: H[:, kc, :], Hb, px[:, :], pxb)
            if hf == 0:
                ph, phb = self.ps("gl")
                self.proj(wc, lambda kc: self.HH[:, kc, :], [self.hhbuf], ph[:, 0:HALO], phb)
                self.proj(wx, lambda kc: self.HH[:, kc, :], [self.hhbuf], ph[:, 64:64 + HALO], phb)
                th, thb = self.tf.get()
                self.act(th[:, 0:HALO], ph[:, 0:HALO], AF.Copy, [phb], [thb])
                self.tt(ub[:, 0:HALO], th[:, 0:HALO], ph[:, 64:64 + HALO], ALU.mult, [thb, phb], [ubb])
            else:
                self.act(ub[:, 0:HALO], self.HALOB[:, j, :], AF.Copy, [self.halobuf[j]], [ubb])
            tcx, tcxb = self.tf.get()
            self.act(tcx[:, :], pc[:, :], AF.Copy, [pcb], [tcxb])
            self.tt(ub[:, HALO:HALO + TH], tcx[:, :], px[:, :], ALU.mult, [tcxb, pxb], [ubb])
            if hf == 0:
                self.act(self.HALOB[:, j, :], ub[:, TH:TH + HALO], AF.Copy, [ubb], [self.halobuf[j]])
            acc, accb = self.tf.get()
            self.ts(acc[:, :], ub[:, HALO - 2:HALO - 2 + TH], vec[:, V_SCW + j:V_SCW + j + 1], vec[:, V_SCB + j:V_SCB + j + 1],
                    ALU.mult, ALU.add, [ubb, self.vbuf], [accb])
            for k in (1, 2):
                self.stt(acc[:, :], ub[:, HALO - 2 + k:HALO - 2 + k + TH], vec[:, V_SCW + k * 8 + j:V_SCW + k * 8 + j + 1],
                         acc[:, :], ALU.mult, ALU.add, [ubb, accb, self.vbuf], [accb])
            pb, pbb = self.ps("mm")
            self.proj(wb_, lambda kc: H[:, kc, :], Hb, pb[:, :], pbb)
            self.tt(self.A[:, j, :], acc[:, :], pb[:, :], ALU.mult, [accb, pbb], [self.abuf])
        pmu, pmub = self.ps("ax")
        psq, psqb = self.ps("ax")
        ubs = {}

        def stage_a(j):
            wg = W.load(("in", C_CFG + 128 * j))
            wa = W.load(("in", C_CFA + 128 * j))
            ub, ubb = self.ub16p.get()
            ubs[j] = (ub, ubb)
            pg, pgb = self.ps("mm")
            self.proj(wg, lambda kc: H[:, kc, :], Hb, pg[:, :], pgb)
            pa, pab = self.ps("mm")
            self.proj(wa, lambda kc: H[:, kc, :], Hb, pa[:, :], pab)
            if hf == 0:
                ph, phb = self.ps("gl")
                self.proj(wg, lambda kc: self.HH[:, kc, :], [self.hhbuf], ph[:, 0:HALO], phb)
                self.proj(wa, lambda kc: self.HH[:, kc, :], [self.hhbuf], ph[:, 64:64 + HALO], phb)
                th, thb = self.tf.get()
                self.act(th[:, 0:HALO], ph[:, 0:HALO], AF.Sigmoid, [phb], [thb])
                self.tt(ub[:, 0:HALO], th[:, 0:HALO], ph[:, 64:64 + HALO], ALU.mult, [thb, phb], [ubb])
            else:
                self.act(ub[:, 0:HALO], self.HALOB[:, 8 + j, :], AF.Copy, [self.halobuf[8 + j]], [ubb])
            tg, tgb = self.tf.get()
            self.act(tg[:, :], pg[:, :], AF.Sigmoid, [pgb], [tgb])
            self.tt(ub[:, HALO:HALO + TH], tg[:, :], pa[:, :], ALU.mult, [tgb, pab], [ubb])
            if hf == 0:
                self.act(self.HALOB[:, 8 + j, :], ub[:, TH:TH + HALO], AF.Copy, [ubb], [self.halobuf[8 + j]])

        def stage_b(j):
            ub, ubb = ubs.pop(j)
            pcv, pcvb = self.ps("gl")
            for k in range(31):
                dg, dgb = self.diagp.get()
                self.ts(dg, self.IDB[:, :], vec[:, V_CFW + k * 8 + j:V_CFW + k * 8 + j + 1], None, ALU.mult, None,
                        [self.cbuf, self.vbuf], [dgb])
                P.op("pe", lambda hh, k=k, dg=dg, ub=ub, pcv=pcv: hh.matmul(pcv[:, :], lhsT=dg, rhs=ub[:, 2 + k:2 + k + TH],
                                                                           start=(k == 0), stop=(k == 30)),
                     reads=[dgb, ubb], writes=[pcvb])
            U = self.UCF[:, j, :]
            Ub = self.ucfbuf[j]
            self.act(U, pcv[:, :], AF.Identity, [pcvb, self.vbuf], [Ub], bias=vec[:, V_CFB + j:V_CFB + j + 1])
            u16, u16b = self.tb.get()
            self.act(u16[:, :], U, AF.Copy, [Ub], [u16b])
            usq, usqb = self.tb.get()
            self.act(usq[:, :], U, AF.Square, [Ub], [usqb])
            return (u16, u16b, usq, usqb)

        def stage_c(j, st):
            u16, u16b, usq, usqb = st
            P.op("pe", lambda hh: hh.matmul(pmu[:, :], lhsT=self.ones_cf[:], rhs=u16[:, :], start=(j == 0), stop=(j == 7)),
                 reads=[u16b, self.cbuf], writes=[pmub])
            P.op("pe", lambda hh: hh.matmul(psq[:, :], lhsT=self.ones_cf[:], rhs=usq[:, :], start=(j == 0), stop=(j == 7)),
                 reads=[usqb, self.cbuf], writes=[psqb])

        stats = {}
        stage_a(0)
        for j in range(8):
            if j + 1 < 8:
                stage_a(j + 1)
            stats[j] = stage_b(j)
            if state_cb is not None:
                state_cb(j)
            if j >= 1:
                stage_c(j - 1, stats.pop(j - 1))
        stage_c(7, stats.pop(7))
        mu, mub = self.MU, self.mubuf
        var, varb = self.VAR, self.varbuf
        self.act(mu, pmu[:, :], AF.Copy, [pmub], [mub])
        self.tt(var, mu, mu, ALU.mult, [mub], [varb])
        self.tt(var, psq[:, :], var, ALU.subtract, [psqb, varb], [varb])
        self.ts(var, var, 0.0, None, ALU.max, None, [varb], [varb])
        self.act(var, var, AF.Ln, [varb], [varb], bias=self.eps_ap[:, 0:1])
        self.act(var, var, AF.Exp, [varb], [varb], scale=-0.5)
        self.stt(mu, mu, -1.0, var, ALU.mult, ALU.mult, [mub, varb], [mub])
        for j in range(8):
            t1, t1b = self.tf.get()
            self.tt(t1[:, :], self.UCF[:, j, :], var, ALU.mult, [self.ucfbuf[j], varb], [t1b])
            self.tt(t1[:, :], t1[:, :], mu, ALU.add, [t1b, mub], [t1b])
            self.act(self.B[:, j, :], t1[:, :], AF.Silu, [t1b, self.vbuf], [self.bbuf],
                     scale=vec[:, V_LNG + j:V_LNG + j + 1], bias=vec[:, V_LNB + j:V_LNB + j + 1])
        if not self.use_stash:
            P.barrier()
        if not self.use_stash:
            self.glr_proj(W, H, Hb)
        for h in range(NH):
            self.gla_head(W, h, hf, H, Hb, vec, state_only=False)
        if not self.use_stash:
            P.barrier()
        for n in range(NCH):
            acc, accb = self.tf.get()
            wab = W.load(("outab", n))
            for bi in range(3):
                wmt = W.load(("in", (C_MA, C_MB, C_MC)[bi] + 128 * n))
                if bi == 2:
                    wgl = W.load(("glaout", n))
                wt, kc0, nkc, SRC, srcb = [(wab, 0, 8, self.A, self.abuf), (wab, 8, 8, self.B, self.bbuf),
                                            (wgl if bi == 2 else None, 0, 16, self.G, self.gbuf)][bi]
                pu, pub = self.ps("mm")
                self.proj(wt, lambda kc, SRC=SRC: SRC[:, kc, :], [srcb], pu[:, :], pub, kc0=kc0, nkc=nkc)
                pm, pmb = self.ps("mm")
                self.proj(wmt, lambda kc: H[:, kc, :], Hb, pm[:, :], pmb)
                sg, sgb = self.tf.get()
                self.act(sg[:, :], pm[:, :], AF.Sigmoid, [pmb], [sgb])
                if bi == 0:
                    self.tt(acc[:, :], sg[:, :], pu[:, :], ALU.mult, [sgb, pub], [accb])
                else:
                    self.tt(sg[:, :], sg[:, :], pu[:, :], ALU.mult, [sgb, pub], [sgb])
                    if bi == 1:
                        self.tt(acc[:, :], acc[:, :], sg[:, :], ALU.add, [accb, sgb], [accb])
                    else:
                        mg_alias = ([self.obuf] if n < 8 else [self.sgbuf[n - 8]] if n < 12 else
                                    [self.kkbuf[n - 12]] if n < 14 else [self.qqbuf[n - 14]]) if self.use_stash else []
                        self.tt(self.MG[:, n, :], acc[:, :], sg[:, :], ALU.add, [accb, sgb], [self.mgbuf] + mg_alias)
        for n in range(NCH):
            wo = W.load(("wo", n))
            po, pob = self.ps("mm")
            self.proj(wo, lambda kc: self.MG[:, kc, :], [self.mgbuf], po[:, :], pob)
            self.tt(X[:, n, cs], X[:, n, cs], po[:, :], ALU.add, [XB[n][hf], pob], [XB[n][hf]])
        P.barrier()

    def ffn_ple(self, W, vec, pT):
        P = self.P
        X, XB = self.X, self.XB
        H2 = self.H2
        for hf in range(2):
            cs = slice(hf * TH, (hf + 1) * TH)
            self.rmsnorm([X[:, c, cs] for c in range(NCH)], [XB[c][hf] for c in range(NCH)], V_GFFN,
                         [H2[:, c, cs] for c in range(NCH)], self.h2buf[hf], TH, vec)
        for fg in range(NFG):
            HID = self.HID
            hidb = self.hidbuf
            for f in range(FGS):
                wg = W.load(("gate", fg * FGS + f))
                wu = W.load(("up", fg * FGS + f))
                for hf in range(2):
                    cs = slice(hf * TH, (hf + 1) * TH)
                    pg, pgb = self.ps("mm")
                    self.proj(wg, lambda kc, cs=cs: H2[:, kc, cs], self.h2buf[hf], pg[:, :], pgb)
                    pu, pub = self.ps("mm")
                    self.proj(wu, lambda kc, cs=cs: H2[:, kc, cs], self.h2buf[hf], pu[:, :], pub)
                    sg, sgb = self.tf.get()
                    self.act(sg[:, :], pg[:, :], AF.Silu, [pgb], [sgb])
                    self.tt(HID[:, f, cs], sg[:, :], pu[:, :], ALU.mult, [sgb, pub], [hidb[hf]])
            for n in range(NCH):
                wd = W.load(("down", fg, n), ncols=FGS * 128)
                w3, wb = self.w3(wd, kc=FGS)
                for hf in range(2):
                    cs = slice(hf * TH, (hf + 1) * TH)
                    pd, pdb = self.ps("mm")
                    pairs = [(w3[:, f, :], HID[:, f, cs], [wb, hidb[hf]]) for f in range(FGS)]
                    self.mm_group(pd[:, :], pdb, pairs)
                    self.tt(X[:, n, cs], X[:, n, cs], pd[:, :], ALU.add, [XB[n][hf], pdb], [XB[n][hf]])
        for hf in range(2):
            cs = slice(hf * TH, (hf + 1) * TH)
            self.rmsnorm([X[:, c, cs] for c in range(NCH)], [XB[c][hf] for c in range(NCH)], V_GPLE,
                         [H2[:, c, cs] for c in range(NCH)], self.h2buf[hf], TH, vec)
        hb2 = [self.hidbuf[0], self.hidbuf[1]]
        for kc in range(2):
            P.dma("pool", lambda h, kc=kc: h.dma_start(out=self.PT[:, kc, :], in_=pT[kc * 128:(kc + 1) * 128, :]), self.pt_sem,
                  writes=hb2)
        for g in range(2):
            idx = W.index[("ple", g)]
            dram = W.dram
            P.dma("pool", lambda h, g=g, idx=idx: h.dma_start(out=self.PLEW[:, g * 2048:(g + 1) * 2048], in_=dram[idx, :, :]),
                  self.plew_sem, writes=hb2)
        plew = self.PLEW.rearrange("p (n k j) -> p n k j", k=2, j=128)
        for n in range(NCH):
            wpg = W.load(("plegate", n))
            for hf in range(2):
                cs = slice(hf * TH, (hf + 1) * TH)
                pg, pgb = self.ps("mm")
                self.proj(wpg, lambda kc, cs=cs: H2[:, kc, cs], self.h2buf[hf], pg[:, :], pgb)
                pp, ppb = self.ps("mm")
                pairs = [(plew[:, n, kc, :], self.PT[:, kc, cs], hb2) for kc in range(2)]
                self.mm_group(pp[:, :], ppb, pairs)
                sg, sgb = self.tf.get()
                self.act(sg[:, :], pg[:, :], AF.Sigmoid, [pgb], [sgb])
                self.tt(sg[:, :], sg[:, :], pp[:, :], ALU.mult, [sgb, ppb], [sgb])
                self.tt(X[:, n, cs], X[:, n, cs], sg[:, :], ALU.add, [XB[n][hf], sgb], [XB[n][hf]])

    def prepass(self, W, vec, wgk_dram, mid_cb=None):
        P = self.P
        X, XB = self.X, self.XB
        H, Hb = self.H, self.Hbuf
        P.dma("pool", lambda h: h.dma_start(out=self.WGK[:, :], in_=wgk_dram[:, :]), self.nsem("wgkp"), writes=[self.cbuf])
        P.op("dve", lambda h: h.memset(self.S[:, :, :], 0.0), writes=self.Sbuf)
        P.op("dve", lambda h: h.memset(self.DACC[:, :], 1.0), writes=[self.daccbuf])
        for hf in range(2):
            cs = slice(hf * TH, (hf + 1) * TH)
            self.rmsnorm([X[:, c, cs] for c in range(NCH)], [XB[c][hf] for c in range(NCH)], V_GMIX,
                         [H[:, c, :] for c in range(NCH)], Hb, TH, vec)
            self.glr_proj(W, H, Hb)
            for h in range(NH):
                self.gla_head(W, h, hf, H, Hb, vec, state_only=True)
            if hf == 0 and mid_cb is not None:
                mid_cb()
        P.barrier()

    def halo_norm(self, vec):
        self.rmsnorm([self.XH[:, c, :] for c in range(NCH)], [self.xhbuf] * NCH, V_GMIX,
                     [self.HH[:, c, :] for c in range(NCH)], [self.hhbuf] * NCH, HALO, vec)

    def misc_sem(self):
        return self.P.new_dma_sem()

    def nsem(self, name):
        if name not in self._nsems:
            self._nsems[name] = self.P.new_dma_sem()
        return self._nsems[name]

    def build(self):
        mode = self.mode
        nc = self.nc
        fused = mode == "fused"
        self.use_stash = fused
        has_main = mode in ("mainpre", "mainfinal", "fused")
        has_pre = mode in ("pre", "mainpre")
        has_final = mode in ("mainfinal", "fused")
        mkeys = main_keys()
        pkeys = pre_keys()
        PK = 8 * DV + NCH * HALO + 8
        xT = self.dram_in("xT", [D, T])
        const_d = self.dram_in("const", [128, 512])
        if fused:
            pT4 = self.dram_in("pT4", [DEPTH, PLE, T])
            was = [self.dram_in("wa%d" % l, [len(mkeys), 128, 2048]) for l in range(DEPTH)]
            vec4 = self.dram_in("vec4", [DEPTH, 128, NV])
            wgk4 = self.dram_in("wgk4", [DEPTH, 32, 1024])
            rmask = self.dram_in("rmask", [128, 9])
            self.st_kd = nc.dram_tensor("st_kd", [8, 128, 4 * DK], BF16).ap()
            self.st_vt = nc.dram_tensor("st_vt", [8, 128, 4 * DV], BF16).ap()
            self.st_eb = nc.dram_tensor("st_eb", [8, 128, 2 * TH], F32).ap()
            self.st_enb = nc.dram_tensor("st_enb", [8, 2, 128, TH], F32).ap()
            self.stbuf = [[Buf("st%d_%d" % (a, k)) for k in range(5)] for a in range(8)]
            XC = (4 * DV, 4 * DV, NCH * HALO, 8)
            xsrc = [[nc.dram_tensor("xsrc%d_%d" % (i, k), [128, XC[k]], F32).ap() for k in range(4)] for i in range(2)]
            xdst = [[nc.dram_tensor("xdst%d_%d" % (i, k), [4 * 128, XC[k]], F32).ap() for k in range(4)] for i in range(2)]
        elif has_main:
            xh = self.dram_in("xh", [D, HALO])
            pT = self.dram_in("pT", [PLE, T])
            sall = self.dram_in("sall", [3, 8, 128, DV])
            dall = self.dram_in("dall", [128, 24])
            rmask = self.dram_in("rmask", [128, 9])
            wa = self.dram_in("wa", [len(mkeys), 128, 2048])
            vec_d = self.dram_in("vec", [128, NV])
            wgk_d = self.dram_in("wgk", [32, 1024])
        if has_pre:
            wp = self.dram_in("wp", [len(pkeys), 128, 2048])
            vecp_d = self.dram_in("vecp", [128, 16])
            wgkp_d = self.dram_in("wgkp", [32, 1024])
            sloc = self.dram_out("sloc", [8, 128, DV])
            dloc = self.dram_out("dloc", [128, 8])
        if mode == "mainpre":
            xTo = self.dram_out("xTo", [D, T])
        if has_final:
            outT = self.dram_out("outT", [D, T])

        with ExitStack() as es:
            self.es = es
            P = self.P = Prog(nc, es)
            self.X = self.sb("X", [128, NCH, T], F32)
            self.XB = [[Buf("X%d_%d" % (c, hf)) for hf in range(2)] for c in range(NCH)]
            self.VEC = self.sb("VEC", [128, NV], F32)
            if has_pre:
                self.VECP = self.sb("VECP", [128, 16], F32)
            self.vbuf = Buf("vec")
            self.CONST = self.sb("CONST", [128, 512], F32)
            self.IDB = self.sb("IDB", [128, 128], BF16)
            self.WGK = self.sb("WGK", [32, 1024], BF16)
            self.GLR = self.sb("GLR", [32, TH], BF16)
            self.cbuf = Buf("const")
            self.glrbuf = Buf("glr")
            self.ones_d = self.sb("ones_d", [128, 128], BF16)
            self.ones_cf = self.sb("ones_cf", [128, 128], BF16)
            self.ones_dv = self.sb("ones_dv", [128, 128], BF16)
            self.eps_ap = self.sb("eps", [128, 1], F32)
            self.one_ap = self.sb("one", [128, 1], F32)
            self.S = self.sb("S", [128, 8, DV], F32)
            self.Sbuf = [Buf("S%d" % a) for a in range(8)]
            self.DACC = self.sb("DACC", [128, 8], F32)
            self.daccbuf = Buf("dacc")
            self.RMASK = self.sb("RMASK", [128, 9], F32)
            self.DALL = self.sb("DALL", [128, 24], F32)
            self.DPALL = self.sb("DPALL", [128, 24], F32)
            self.dpbuf = Buf("dpall")
            self.smallbuf = Buf("small")
            self.rmbuf = Buf("rmask")
            self.HALOB = self.sb("HALOB", [128, 16, HALO], F32)
            self.halobuf = [Buf("halo%d" % j) for j in range(16)]
            self.ring = [self.sb("ring%d" % s_, [128, 2048], BF16) for s_ in range(NSLOT)]
            self.ring_buf = [Buf("ring%d" % s_) for s_ in range(NSLOT)]
            self.ring_sem = [P.new_dma_sem() for s_ in range(NSLOT)]
            self.ring_gen = [0] * NSLOT
            self.ring_i = 0
            tfn, tbn = 5, 4
            self.tf = Pool([self.sb("tf%d" % i, [128, TH], F32) for i in range(tfn)])
            self.tb = Pool([self.sb("tb%d" % i, [128, TH], BF16) for i in range(tbn)])
            banks = [es.enter_context(nc.psum_tensor("ps%d" % i, [128, 512], F32)) for i in range(8)]
            self.pspool = {"mm": Pool(banks[0:4]), "ax": Pool(banks[4:6]), "gl": Pool(banks[6:8])}
            AW = 20480 if has_main else 7680
            self.arena = self.sb("arena", [128, AW], F32)

            def af(off, n):
                assert off + n <= AW
                return self.arena[:, off:off + n]

            def ab(off, n):
                return af(off, n).bitcast(BF16)

            self.H = ab(0, 4096).rearrange("p (c t) -> p c t", t=TH)
            self.Hbuf = KcBufs(Buf("H%d" % c) for c in range(NCH))
            self.GT = af(4096, 1024).rearrange("p (t d) -> p t d", d=DK)
            self.gtbuf = [Buf("gt%d" % t) for t in range(4)]
            self.EB = af(5120, 1024).rearrange("p (i t) -> p i t", t=TH)
            self.ebbuf = [Buf("eb%d" % i) for i in range(2)]
            self.KDEC = ab(6144, 512).rearrange("p (t d) -> p t d", d=DK)
            self.kdbuf = [Buf("kd%d" % t) for t in range(4)]
            self.VT = ab(6656, 1024).rearrange("p (t d) -> p t d", d=DV)
            self.vtbuf = [Buf("vt%d" % t) for t in range(4)]
            self.RS = af(4096, 512)
            if has_main:
                self.ubp = Pool([af(4096, HALO + TH), af(4096 + 544, HALO + TH)])
                self.ub16p = Pool([ab(4096 + 1088, 272), ab(4096 + 1360, 272)])
                self.diagp = Pool([ab(4096 + 1632 + 64 * i, 64) for i in range(6)])
                M0 = 7680
                self.SB16 = ab(M0, 512).rearrange("p (i t) -> p i t", t=DV)
                self.sb16buf = [Buf("sb16_%d" % i) for i in range(2)]
                self.A = ab(M0 + 512, 2048).rearrange("p (c t) -> p c t", t=TH)
                self.abuf = Buf("A")
                self.B = ab(M0 + 2560, 2048).rearrange("p (c t) -> p c t", t=TH)
                self.bbuf = Buf("B")
                R1 = M0 + 4608
                self.UCF = af(R1, 4096).rearrange("p (c t) -> p c t", t=TH)
                self.ucfbuf = [Buf("ucf%d" % j) for j in range(8)]
                self.G = ab(R1, 4096).rearrange("p (c t) -> p c t", t=TH)
                self.gbuf = Buf("G")
                R2 = R1 + 4096
                self.XHF = af(R2, 512)
                self.XH = af(R2, 512).rearrange("p (c t) -> p c t", t=HALO)
                self.xhbuf = Buf("XH")
                self.HH = ab(R2 + 512, 256).rearrange("p (c t) -> p c t", t=HALO)
                self.hhbuf = Buf("HH")
                self.MU = af(R2 + 1024, 512)
                self.VAR = af(R2 + 1536, 512)
                self.mubuf = Buf("mu")
                self.varbuf = Buf("var")
                self.O = af(R2, 2048).rearrange("p (e t) -> p e t", t=TH)
                self.obuf = Buf("O")
                self.SG = ab(R2 + 2048, 1024).rearrange("p (i t) -> p i t", t=TH)
                self.sgbuf = [Buf("sg%d" % i) for i in range(4)]
                self.KK = ab(R2 + 3072, 512).rearrange("p (i t) -> p i t", t=TH)
                self.kkbuf = [Buf("kk%d" % i) for i in range(2)]
                self.QQ = ab(R2 + 3584, 512).rearrange("p (i t) -> p i t", t=TH)
                self.qqbuf = [Buf("qq%d" % i) for i in range(2)]
                self.MG = ab(R2, 4096).rearrange("p (c t) -> p c t", t=TH)
                self.mgbuf = Buf("MG")
                assert R2 + 4096 == AW
                self.H2 = ab(0, 8192).rearrange("p (c t) -> p c t", t=T)
                self.h2buf = [KcBufs(Buf("h2_%d_%d" % (hf, c)) for c in range(NCH)) for hf in range(2)]
                self.HID = ab(8192, 5632).rearrange("p (f t) -> p f t", t=T)
                self.hidbuf = [Buf("hid_%d" % hf) for hf in range(2)]
                self.PT = ab(8192, 1024).rearrange("p (k t) -> p k t", t=T)
                self.PLEW = ab(8192 + 1024, 2048)
                self.plew_sem = P.new_dma_sem()
                self.pt_sem = P.new_dma_sem()
                self.stgp = Pool([af(R2 + 2048 + 512 * i, 512) for i in range(4)])
                self.al_eb = [self.ubp.bufs[1], self.ub16p.bufs[0], self.ub16p.bufs[1]] + list(self.diagp.bufs)
                self.al_rs = [self.ubp.bufs[0]]
                self.al_sg = [self.stgp.bufs[0], self.stgp.bufs[0], self.stgp.bufs[1], self.stgp.bufs[1]]
                self.al_kk = [self.stgp.bufs[2], self.stgp.bufs[2]]
                self.al_qq = [self.stgp.bufs[3], self.stgp.bufs[3]]
                self.al_o = [self.xhbuf, self.hhbuf, self.mubuf, self.varbuf]
                self.FST = [af(512 * i, 512) for i in range(16)]

            s0 = P.new_dma_sem()
            P.dma("sp", lambda h: h.dma_start(out=self.CONST[:, :], in_=const_d[:, :]), s0, writes=[self.cbuf])
            P.op("dve", lambda h: h.memset(self.ones_d[:, :], 1.0 / D), writes=[self.cbuf])
            P.op("dve", lambda h: h.memset(self.ones_cf[:, :], 1.0 / 1024.0), writes=[self.cbuf])
            P.op("dve", lambda h: h.memset(self.ones_dv[:, :], 1.0 / DV), writes=[self.cbuf])
            P.op("dve", lambda h: h.memset(self.eps_ap[:, :], EPS), writes=[self.cbuf])
            P.op("dve", lambda h: h.memset(self.one_ap[:, :], 1.0), writes=[self.cbuf])
            P.op("dve", lambda h: h.memset(self.GLR[:, :], 1.0), writes=[self.glrbuf])
            P.op("dve", lambda h: h.tensor_copy(out=self.IDB[:, :], in_=self.CONST[:, 384:512]), reads=[self.cbuf], writes=[self.cbuf])
            xsem = [P.new_dma_sem() for _ in range(8)]
            xTv = xT.rearrange("(c p) t -> p c t", p=128)
            for hf_ in range(2):
                for g in range(4):
                    P.dma("sp", lambda h, g=g, hf_=hf_: h.dma_start(out=self.X[:, 4 * g:4 * g + 4, hf_ * TH:(hf_ + 1) * TH],
                                                                   in_=xTv[:, 4 * g:4 * g + 4, hf_ * TH:(hf_ + 1) * TH]),
                          xsem[hf_ * 4 + g], writes=[self.XB[c][hf_] for c in range(4 * g, 4 * g + 4)])
            self.out_toks = []

            def build_xh(xh_srcs, src_bufs=()):
                if xh_srcs is None:
                    P.dma("sp", lambda h: h.dma_start(out=self.XH, in_=xh.rearrange("(c p) t -> p c t", p=128)), self.nsem("xh"),
                          writes=[self.xhbuf])
                    return
                P.op("dve", lambda h: h.memset(self.XHF, 0.0), writes=[self.xhbuf])
                for r in range(3):
                    t1, t1b = self.tf.get()
                    P.dma("sp", lambda h, r=r, t1=t1: h.dma_start(out=t1[:, :], in_=xh_srcs(r)), self.nsem("xhr%d" % r),
                          reads=list(src_bufs), writes=[t1b])
                    self.stt(self.XHF, t1[:, :], self.RMASK[:, 6 + r:7 + r], self.XHF, ALU.mult, ALU.add,
                             [t1b, self.rmbuf, self.xhbuf], [self.xhbuf])

            def combine_begin(dall_src, src_bufs=()):
                for r in range(3):
                    P.dma("sp", lambda h, r=r: h.dma_start(out=self.DALL[:, r * 8:(r + 1) * 8], in_=dall_src(r)), self.nsem("dall"),
                          reads=list(src_bufs), writes=[self.smallbuf])

            def combine_tile(a, stg_src, src_bufs=()):
                if a == 0:
                    P.op("dve", lambda h: h.memset(self.S[:, :, :], 0.0), writes=self.Sbuf)
                    for r in range(3):
                        self.ts(self.DPALL[:, r * 8:(r + 1) * 8], self.DALL[:, r * 8:(r + 1) * 8], self.RMASK[:, r:r + 1],
                                self.RMASK[:, 3 + r:4 + r], ALU.mult, ALU.add, [self.smallbuf, self.rmbuf], [self.dpbuf])
                for r in range(3):
                    st, stb = self.stgp.get()
                    P.dma("sp", lambda h, r=r, st=st: h.dma_start(out=st, in_=stg_src(r, a)), self.nsem("stg%d" % ((self.stgp.i - 1) % 4)),
                          reads=list(src_bufs), writes=[stb])
                    self.ts(st, st, self.RMASK[:, r:r + 1], None, ALU.mult, None, [stb, self.rmbuf], [stb])
                    self.stt(self.S[:, a, :], self.S[:, a, :], self.DPALL[:, r * 8 + a:r * 8 + a + 1], st, ALU.mult, ALU.add,
                             [self.Sbuf[a], self.dpbuf, stb], [self.Sbuf[a]])

            if has_main:
                P.dma("sp", lambda h: h.dma_start(out=self.RMASK[:, :], in_=rmask[:, :]), self.nsem("rmask"), writes=[self.rmbuf])

            if has_main and not fused:
                P.dma("sp", lambda h: h.dma_start(out=self.VEC[:, :], in_=vec_d[:, :]), self.nsem("vec"), writes=[self.vbuf])
                P.dma("pool", lambda h: h.dma_start(out=self.WGK[:, :], in_=wgk_d[:, :]), self.nsem("wgkp"), writes=[self.cbuf])
                build_xh(None)
                self.halo_norm(self.VEC)
                W = WStream(self, wa, mkeys)
                combine_begin(lambda r: dall[:, r * 8:(r + 1) * 8])
                self.mixer_half(W, 0, self.VEC, state_cb=lambda a: combine_tile(a, lambda r, a: sall[r, a]))
                self.mixer_half(W, 1, self.VEC)
                self.ffn_ple(W, self.VEC, pT)
                P.barrier()

            if fused:
                xs_bufs = [[Buf("xsrc%d_%d" % (i, k)) for k in range(4)] for i in range(2)]
                xd_buf = [[Buf("xdst%d_%d" % (i, k)) for k in range(4)] for i in range(2)]
                ccsem = P.new_dma_sem()
                GRP = [[0, 1, 2, 3], [4, 5, 6, 7]]

                def allgather(i, k):
                    P.dma("pool", lambda h, i=i, k=k: h.collective_compute("AllGather", ALU.bypass, replica_groups=GRP,
                                                                           ins=[xsrc[i][k][:, :]], outs=[xdst[i][k][:, :]]),
                          ccsem, reads=[xs_bufs[i][k]], writes=[xd_buf[i][k]], inc=1)

                for l in range(DEPTH):
                    i = l % 2
                    W = WStream(self, was[l], mkeys)
                    P.dma("sp", lambda h, l=l: h.dma_start(out=self.VEC[:, :], in_=vec4[l]), self.nsem("vec"), writes=[self.vbuf])
                    P.dma("sp", lambda h, i=i: h.dma_start(out=xsrc[i][2][:, :].rearrange("p (c t) -> p c t", t=HALO),
                                                          in_=self.X[:, :, T - HALO:T]),
                          self.nsem("xsH"), reads=[self.XB[c][1] for c in range(NCH)], writes=[xs_bufs[i][2]])
                    allgather(i, 2)
                    def mid(i=i):
                        build_xh(lambda r: xdst[i][2][r * 128:(r + 1) * 128, :], src_bufs=[xd_buf[i][2]])
                        self.halo_norm(self.VEC)
                    self.prepass(W, self.VEC, wgk4[l], mid_cb=mid)
                    for k in range(2):
                        P.dma("sp", lambda h, i=i, k=k: h.dma_start(out=xsrc[i][k][:, :].rearrange("p (a e) -> p a e", e=DV),
                                                                   in_=self.S[:, 4 * k:4 * k + 4, :]),
                              self.nsem("xsS%d" % k), reads=self.Sbuf[4 * k:4 * k + 4], writes=[xs_bufs[i][k]])
                    P.dma("sp", lambda h, i=i: h.dma_start(out=xsrc[i][3][:, :], in_=self.DACC[:, :]),
                          self.nsem("xsD"), reads=[self.daccbuf], writes=[xs_bufs[i][3]])
                    allgather(i, 3)
                    allgather(i, 0)
                    allgather(i, 1)
                    combine_begin(lambda r, i=i: xdst[i][3][r * 128:(r + 1) * 128, :], src_bufs=[xd_buf[i][3]])
                    self.mixer_half(W, 0, self.VEC, state_cb=lambda a, i=i: combine_tile(
                        a, lambda r, a: xdst[i][a // 4][r * 128:(r + 1) * 128, (a % 4) * DV:(a % 4 + 1) * DV],
                        src_bufs=[xd_buf[i][a // 4]]))
                    self.mixer_half(W, 1, self.VEC)
                    self.ffn_ple(W, self.VEC, pT4[l])
                    P.barrier()

            if mode == "mainpre":
                so = [P.new_dma_sem() for _ in range(4)]
                xTov = xTo.rearrange("(c p) t -> p c t", p=128)
                for g in range(4):
                    tk = P.dma("sp", lambda h, g=g: h.dma_start(out=xTov[:, 4 * g:4 * g + 4, :], in_=self.X[:, 4 * g:4 * g + 4, :]), so[g],
                               reads=[self.XB[c][hf] for c in range(4 * g, 4 * g + 4) for hf in range(2)])
                    self.out_toks.append(tk)

            if has_pre:
                s8 = P.new_dma_sem()
                P.dma("sp", lambda h: h.dma_start(out=self.VECP[:, :], in_=vecp_d[:, :]), s8, writes=[self.vbuf])
                Wp = WStream(self, wp, pkeys)
                self.prepass(Wp, self.VECP, wgkp_d)
                s9 = P.new_dma_sem()
                tk = P.dma("sp", lambda h: h.dma_start(out=sloc.rearrange("a p e -> p a e"), in_=self.S[:, :, :]), s9,
                           reads=self.Sbuf)
                self.out_toks.append(tk)
                s10 = P.new_dma_sem()
                tk = P.dma("sp", lambda h: h.dma_start(out=dloc[:, :], in_=self.DACC[:, :]), s10, reads=[self.daccbuf])
                self.out_toks.append(tk)

            if has_final:
                fst = Pool(self.FST)
                fst_sem = [P.new_dma_sem() for _ in range(16)]
                outTv = outT.rearrange("(c p) t -> p c t", p=128)
                for hf in range(2):
                    cs = slice(hf * TH, (hf + 1) * TH)
                    pst, psb = self.ps("ax")
                    for c in range(NCH):
                        sq, sqb = self.tb.get()
                        self.act(sq[:, :], self.X[:, c, cs], AF.Square, [self.XB[c][hf]], [sqb])
                        P.op("pe", lambda h, c=c, sq=sq, pst=pst: h.matmul(pst[:, :], lhsT=self.ones_d[:], rhs=sq[:, :],
                                                                          start=(c == 0), stop=(c == NCH - 1)),
                             reads=[sqb, self.cbuf], writes=[psb])
                    rs, rsb = self.tf.get()
                    self.act(rs[:, :], pst[:, :], AF.Ln, [psb], [rsb], bias=self.eps_ap[:, 0:1])
                    self.act(rs[:, :], rs[:, :], AF.Exp, [rsb], [rsb], scale=-0.5)
                    for c in range(NCH):
                        st, stb = fst.get()
                        si = (fst.i - 1) % 16
                        self.stt(st, self.X[:, c, cs], self.VEC[:, V_GFIN + c:V_GFIN + c + 1], rs[:, :], ALU.mult, ALU.mult,
                                 [self.XB[c][hf], rsb, self.vbuf], [stb])
                        tk = P.dma("sp", lambda h, c=c, cs=cs, st=st: h.dma_start(out=outTv[:, c, cs], in_=st),
                                   fst_sem[si], reads=[stb])
                        self.out_toks.append(tk)

            P.wait_all("sp", self.out_toks)
            P.emit()
        return nc


_PROGRAMS = {}


def get_program(mode):
    if mode not in _PROGRAMS:
        _PROGRAMS[mode] = K(mode).build()
    return _PROGRAMS[mode]


def _core_tokens(c):
    return c // 4, (c % 4) * T


def _rmask(c):
    j = c % 4
    m = np.zeros((128, 9), np.float32)
    for r in range(3):
        m[:, r] = 1.0 if r < j else 0.0
        m[:, 3 + r] = 0.0 if r < j else 1.0
        m[:, 6 + r] = 1.0 if r == j - 1 else 0.0
    return m


def kernel(**inputs):
    w = {k: np.asarray(v) for k, v in inputs.items()}
    x = w["x"]
    p = w["p"]
    const = make_const()
    cores = list(range(NCORES))
    mkeys = main_keys()
    was = [pack_tiles(w, l, mkeys) for l in range(DEPTH)]
    vec4 = np.stack([pack_vec(w, l) for l in range(DEPTH)], axis=0)
    wgk4 = np.stack([pack_wgk(w, l) for l in range(DEPTH)], axis=0)
    in_maps = []
    for c in cores:
        b, t0 = _core_tokens(c)
        m = {"xT": np.ascontiguousarray(x[b, t0:t0 + T, :].T), "const": const,
             "pT4": np.ascontiguousarray(p[:, b, t0:t0 + T, :].transpose(0, 2, 1)),
             "vec4": vec4, "wgk4": wgk4, "rmask": _rmask(c)}
        for l in range(DEPTH):
            m["wa%d" % l] = was[l]
        in_maps.append(m)
    nc = get_program("fused")
    res = run_bass_kernel_spmd(nc, in_maps, core_ids=cores)
    out = np.empty((2, 4096, D), np.float32)
    for c in cores:
        b, t0 = _core_tokens(c)
        out[b, t0:t0 + T, :] = res.results[c]["outT"].T
    return out
```

```python
import numpy as np
import concourse.bass as bass
import concourse.mybir as mybir
from concourse.bass_utils import run_bass_kernel_spmd
from contextlib import ExitStack

F32 = mybir.dt.float32
BF16 = mybir.dt.bfloat16
AF = mybir.ActivationFunctionType
ALU = mybir.AluOpType

NCORES = 8
DEPTH = 4
D = 2048
NCH = 16
T = 1024
TH = 512
HALO = 32
NH = 4
DK = 256
DV = 512
DFF = 5632
NFG = 4
FGS = 11
PLE = 256
EPS = 1e-6
IN_COLS = 17424
C_SCB, C_SCC, C_SCX, C_CFA, C_CFG = 0, 1024, 2048, 3072, 4096
C_Q, C_K, C_V, C_GO, C_GK = 5120, 6144, 7168, 9216, 11264
C_MA, C_MB, C_MC = 11280, 13328, 15376
NSLOT = 6

V_GMIX, V_GFFN, V_GPLE = 0, 16, 32
V_SCW = 48
V_SCB = 72
V_CFW = 80
V_CFB = 328
V_LNG = 336
V_LNB = 344
V_GNORM = 352
V_GFIN = 356
NV = 372


class Buf:
    __slots__ = ("name", "lw", "rd")

    def __init__(self, name):
        self.name = name
        self.lw = None
        self.rd = {}


class Eng:
    def __init__(self, name, sem):
        self.name = name
        self.sem = sem
        self.count = 0
        self.seen = {}
        self.ops = []


class Prog:
    ENGS = ("pe", "act", "dve", "pool", "sp")

    def __init__(self, nc, es):
        self.nc = nc
        self.es = es
        self.sems = {}
        self.eng = {}
        for e in self.ENGS:
            s = es.enter_context(nc.semaphore("s_" + e))
            self.sems["E" + e] = s
            self.eng[e] = Eng(e, "E" + e)
        self.dma_count = {}
        self.n_dsem = 0
        self.all_dma_toks = {}

    def new_dma_sem(self):
        k = "D%d" % self.n_dsem
        self.n_dsem += 1
        self.sems[k] = self.es.enter_context(self.nc.semaphore("d_%d" % (self.n_dsem - 1)))
        self.dma_count[k] = 0
        return k

    def _need(self, e, tok, waits):
        if tok is None:
            return
        k, v = tok
        if k == e.sem:
            if e.name in ("pe", "sp"):
                return
            if e.count - v >= 2:
                return
        if e.seen.get(k, 0) >= v:
            return
        e.seen[k] = v
        waits.append((k, v))

    def _deps(self, e, reads, writes):
        waits = []
        for b in reads:
            self._need(e, b.lw, waits)
        for b in writes:
            self._need(e, b.lw, waits)
            for k, v in b.rd.items():
                self._need(e, (k, v), waits)
        return waits

    def _mark(self, tok, reads, writes):
        k, v = tok
        for b in reads:
            if b.rd.get(k, 0) < v:
                b.rd[k] = v
        for b in writes:
            b.lw = tok
            b.rd = {}

    def op(self, eng, fn, reads=(), writes=(), inc=True):
        e = self.eng[eng]
        waits = self._deps(e, reads, writes)
        if inc:
            e.count += 1
            tok = (e.sem, e.count)
            incs = [(e.sem, 1)]
        else:
            tok = (e.sem, e.count + 1)
            incs = []
        self._mark(tok, reads, writes)
        e.ops.append((waits, fn, incs))
        return tok

    def dma(self, q, fn, dsem, reads=(), writes=(), inc=16):
        e = self.eng[q]
        waits = self._deps(e, reads, writes)
        self.dma_count[dsem] += inc
        tok = (dsem, self.dma_count[dsem])
        self._mark(tok, reads, writes)
        e.ops.append((waits, fn, [(dsem, inc)]))
        self.all_dma_toks[dsem] = tok
        return tok

    def wait_all(self, eng, toks):
        e = self.eng[eng]
        waits = []
        for t in toks:
            self._need(e, t, waits)
        e.ops.append((waits, None, []))

    def barrier(self):
        toks = [(self.eng[n].sem, self.eng[n].count) for n in ("pe", "act", "dve") if self.eng[n].count > 0]
        toks += list(self.all_dma_toks.values())
        for n in ("pe", "act", "dve", "sp"):
            e = self.eng[n]
            waits = []
            for (k, v) in toks:
                if k == e.sem:
                    continue
                if e.seen.get(k, 0) >= v:
                    continue
                e.seen[k] = v
                waits.append((k, v))
            e.ops.append((waits, None, []))

    def emit(self):
        nc = self.nc
        sems = self.sems

        def run(handle, e):
            for waits, fn, incs in e.ops:
                for k, v in waits:
                    handle.wait_ge(sems[k], v)
                if fn is None:
                    continue
                ins = fn(handle)
                for k, n in incs:
                    ins = ins.then_inc(sems[k], n)

        with nc.Block() as block:
            @block.tensor
            def _(h):
                run(h, self.eng["pe"])

            @block.scalar
            def _(h):
                run(h, self.eng["act"])

            @block.vector
            def _(h):
                run(h, self.eng["dve"])

            @block.gpsimd
            def _(h):
                run(h, self.eng["pool"])

            @block.sync
            def _(h):
                run(h, self.eng["sp"])


class KcBufs(list):
    pass


class Pool:
    def __init__(self, views):
        self.views = views
        self.bufs = [Buf("pool") for _ in views]
        self.i = 0

    def get(self):
        i = self.i % len(self.views)
        self.i += 1
        return self.views[i], self.bufs[i]


def mixer_keys():
    keys = []
    for j in range(8):
        keys += [("in", C_SCC + 128 * j), ("in", C_SCX + 128 * j), ("in", C_SCB + 128 * j)]
    for j in range(8):
        keys += [("in", C_CFG + 128 * j), ("in", C_CFA + 128 * j)]
    keys += [("in", C_GK)]
    for h in range(NH):
        keys += [("in", C_K + DK * h + 128 * i) for i in range(2)]
        keys += [("in", C_Q + DK * h + 128 * i) for i in range(2)]
        keys += [("in", C_V + DV * h + 128 * i) for i in range(4)]
        keys += [("in", C_GO + DV * h + 128 * i) for i in range(4)]
    for n in range(NCH):
        keys += [("outab", n), ("glaout", n), ("in", C_MA + 128 * n), ("in", C_MB + 128 * n), ("in", C_MC + 128 * n)]
    for n in range(NCH):
        keys += [("wo", n)]
    return keys


def main_keys():
    keys = mixer_keys()
    for fg in range(NFG):
        for f in range(FGS):
            keys += [("gate", fg * FGS + f), ("up", fg * FGS + f)]
        for n in range(NCH):
            keys += [("down", fg, n)]
    for n in range(NCH):
        keys += [("plegate", n)]
    keys += [("ple", 0), ("ple", 1)]
    return keys


def pre_keys():
    keys = [("in", C_GK)]
    for h in range(NH):
        keys += [("in", C_K + DK * h + 128 * i) for i in range(2)]
        keys += [("in", C_V + DV * h + 128 * i) for i in range(4)]
    return keys


def _tile_from(M):
    K, nc_ = M.shape
    kc = K // 128
    out = np.zeros((128, 2048), np.float32)
    v = out[:, :kc * 128].reshape(128, kc, 128)
    v[:, :, :nc_] = M.reshape(kc, 128, nc_).transpose(1, 0, 2)
    return out


def pack_tiles(w, l, keys):
    tiles = np.empty((len(keys), 128, 2048), np.float32)
    for i, key in enumerate(keys):
        kind = key[0]
        if kind == "in":
            c0 = key[1]
            ncols = 16 if c0 == C_GK else 128
            tiles[i] = _tile_from(w["w_in"][l][:, c0:c0 + ncols])
        elif kind == "outab":
            n = key[1]
            M = np.concatenate([w["w_sc_out"][l][:, n * 128:(n + 1) * 128], w["w_cf_out"][l][:, n * 128:(n + 1) * 128]], axis=0)
            tiles[i] = _tile_from(M)
        elif kind == "glaout":
            n = key[1]
            tiles[i] = _tile_from(w["w_gla_out"][l][:, n * 128:(n + 1) * 128])
        elif kind == "wo":
            n = key[1]
            tiles[i] = _tile_from(w["w_o"][l][:, n * 128:(n + 1) * 128])
        elif kind == "gate":
            f = key[1]
            tiles[i] = _tile_from(w["w_gate_up"][l][:, f * 128:(f + 1) * 128])
        elif kind == "up":
            f = key[1]
            tiles[i] = _tile_from(w["w_gate_up"][l][:, DFF + f * 128:DFF + (f + 1) * 128])
        elif kind == "down":
            fg, n = key[1], key[2]
            tiles[i] = _tile_from(w["w_down"][l][fg * FGS * 128:(fg + 1) * FGS * 128, n * 128:(n + 1) * 128])
        elif kind == "plegate":
            n = key[1]
            tiles[i] = _tile_from(w["w_ple_gate"][l][:, n * 128:(n + 1) * 128])
        elif kind == "ple":
            g = key[1]
            M = w["w_ple"][l].reshape(2, 128, 16, 128)[:, :, g * 8:(g + 1) * 8, :]
            tiles[i] = np.ascontiguousarray(M.transpose(1, 2, 0, 3)).reshape(128, 2048)
        else:
            raise ValueError(key)
    return tiles


def pack_vec(w, l):
    v = np.zeros((128, NV), np.float32)

    def fm(x):
        return np.asarray(x, np.float32).reshape(-1, 128).T

    v[:, V_GMIX:V_GMIX + 16] = fm(w["g_mix"][l])
    v[:, V_GFFN:V_GFFN + 16] = fm(w["g_ffn"][l])
    v[:, V_GPLE:V_GPLE + 16] = fm(w["g_ple"][l])
    for k in range(3):
        v[:, V_SCW + k * 8:V_SCW + k * 8 + 8] = fm(w["sc_conv_w"][l][k])
    v[:, V_SCB:V_SCB + 8] = fm(w["sc_conv_b"][l])
    for k in range(31):
        v[:, V_CFW + k * 8:V_CFW + k * 8 + 8] = fm(w["cf_conv_w"][l][k])
    v[:, V_CFB:V_CFB + 8] = fm(w["cf_conv_b"][l])
    v[:, V_LNG:V_LNG + 8] = fm(w["cf_ln_g"][l])
    v[:, V_LNB:V_LNB + 8] = fm(w["cf_ln_b"][l])
    v[:, V_GNORM:V_GNORM + 4] = fm(w["g_gla_norm"][l])
    v[:, V_GFIN:V_GFIN + 16] = fm(w["g_final"])
    return v


def pack_wgk(w, l):
    m = np.zeros((32, 1024), np.float32)
    m[0:16] = w["w_gla_gk"][l]
    m[16] = w["b_gla_gk"][l]
    return m


def make_const():
    c = np.zeros((128, 512), np.float32)
    c[:, 384:512] = np.eye(128, dtype=np.float32)
    j = np.arange(128)[:, None]
    i = np.arange(128)[None, :]
    c[:, 0:128] = (j <= i).astype(np.float32)
    c[:, 128:256] = (j <= i).astype(np.float32) * (-1.0 / 16.0)
    c[:, 256:384] = (j > i).astype(np.float32) * (-1.0 / 16.0)
    return c


class WStream:
    def __init__(self, K, dram, keys):
        self.K = K
        self.dram = dram
        self.index = {}
        for i, k in enumerate(keys):
            self.index.setdefault(k, i)

    def load(self, key, ncols=2048):
        K = self.K
        s = K.ring_i % NSLOT
        K.ring_i += 1
        idx = self.index[key]
        slot = K.ring[s]
        dram = self.dram
        K.P.dma("pool", lambda h, s=s, idx=idx, ncols=ncols: h.dma_start(out=slot[:, 0:ncols], in_=dram[idx, :, 0:ncols]),
                K.ring_sem[s], writes=[K.ring_buf[s]])
        K.ring_gen[s] += 1
        return (slot, K.ring_buf[s], s, K.ring_gen[s])


class K:
    def __init__(self, mode):
        self.mode = mode
        self._nsems = {}
        self.nc = bass.Bass("TRN2", target_bir_lowering=False)

    def dram_in(self, name, shape):
        return self.nc.dram_tensor(name, list(shape), F32, kind="ExternalInput").ap()

    def dram_out(self, name, shape):
        return self.nc.dram_tensor(name, list(shape), F32, kind="ExternalOutput").ap()

    def sb(self, name, shape, dtype):
        return self.es.enter_context(self.nc.sbuf_tensor(name, list(shape), dtype))

    def arena_f32(self, nwords):
        off = self.aoff
        self.aoff += nwords
        assert self.aoff <= self.asize, (self.aoff, self.asize)
        return self.arena[:, off:off + nwords]

    def arena_bf16(self, nelem):
        assert nelem % 2 == 0
        return self.arena_f32(nelem // 2).bitcast(BF16)

    def act(self, out, in_, func, reads, writes, **kw):
        return self.P.op("act", lambda h: h.activation(out=out, in_=in_, func=func, **kw), reads, writes)

    def tt(self, out, in0, in1, op, reads, writes, eng="dve"):
        return self.P.op(eng, lambda h: h.tensor_tensor(out=out, in0=in0, in1=in1, op=op), reads, writes)

    def ts(self, out, in0, s1, s2, op0, op1, reads, writes, eng="dve"):
        if s2 is None:
            return self.P.op(eng, lambda h: h.tensor_scalar(out=out, in0=in0, scalar1=s1, scalar2=None, op0=op0), reads, writes)
        return self.P.op(eng, lambda h: h.tensor_scalar(out=out, in0=in0, scalar1=s1, scalar2=s2, op0=op0, op1=op1), reads, writes)

    def stt(self, out, in0, scalar, in1, op0, op1, reads, writes, eng="dve"):
        return self.P.op(eng, lambda h: h.scalar_tensor_tensor(out=out, in0=in0, scalar=scalar, in1=in1, op0=op0, op1=op1),
                         reads, writes)

    def mm_group(self, ps_ap, psbuf, pairs):
        n = len(pairs)
        for i, (l, r, bufs) in enumerate(pairs):
            self.P.op("pe", lambda h, l=l, r=r, i=i: h.matmul(ps_ap, lhsT=l, rhs=r, start=(i == 0), stop=(i == n - 1)),
                      reads=bufs, writes=[psbuf], inc=(i == n - 1))

    def ps(self, pool):
        return self.pspool[pool].get()

    def wuse(self, wt):
        slot, buf, s, gen = wt
        assert self.ring_gen[s] == gen, "weight tile used after its ring slot was reloaded"
        return slot, buf

    def w3(self, wt, kc=16):
        slot, buf = self.wuse(wt)
        return slot[:, 0:kc * 128].rearrange("p (k j) -> p k j", j=128), buf

    def rmsnorm(self, xs, xbufs, gcol, outs, outbufs, ncols, vec, ones=None, out_f32=False):
        ones = self.ones_d if ones is None else ones
        nchunk = len(xs)
        pst, psb = self.ps("ax")
        for c in range(nchunk):
            sq, sqb = self.tb.get()
            self.act(sq[:, 0:ncols], xs[c], AF.Square, [xbufs[c]], [sqb])
            self.P.op("pe", lambda h, c=c, sq=sq: h.matmul(pst[:, 0:ncols], lhsT=ones[:], rhs=sq[:, 0:ncols],
                                                           start=(c == 0), stop=(c == nchunk - 1)),
                      reads=[sqb, self.cbuf], writes=[psb])
        rs, rsb = self.tf.get()
        self.act(rs[:, 0:ncols], pst[:, 0:ncols], AF.Ln, [psb], [rsb], bias=self.eps_ap[:, 0:1])
        self.act(rs[:, 0:ncols], rs[:, 0:ncols], AF.Exp, [rsb], [rsb], scale=-0.5)
        for c in range(nchunk):
            self.stt(outs[c], xs[c], vec[:, gcol + c:gcol + c + 1], rs[:, 0:ncols], ALU.mult, ALU.mult,
                     [xbufs[c], rsb, self.vbuf], [outbufs[c]])

    def proj(self, wt, rhs_of_kc, rhs_bufs, ps_ap, psbuf, kc0=0, nkc=16, lcols=slice(0, 128)):
        w3, wb = self.w3(wt)
        if isinstance(rhs_bufs, KcBufs):
            pairs = [(w3[:, kc0 + kc, lcols], rhs_of_kc(kc), [wb, rhs_bufs[kc]]) for kc in range(nkc)]
        else:
            pairs = [(w3[:, kc0 + kc, lcols], rhs_of_kc(kc), [wb] + list(rhs_bufs)) for kc in range(nkc)]
        self.mm_group(ps_ap, psbuf, pairs)

    def gla_head(self, W, h, hf, H, Hb, vec, state_only):
        P = self.P
        S = self.S
        Sbuf = self.Sbuf
        if (not state_only) and self.use_stash:
            self.gla_head_prep_stash(W, h, hf, H, Hb)
        else:
            self.gla_head_prep_full(W, h, hf, H, Hb, state_only)
        self.gla_head_chunks(h, hf, vec, state_only)

    def gla_head_prep_stash(self, W, h, hf, H, Hb):
        P = self.P
        S, Sbuf = self.S, self.Sbuf
        idx = h * 2 + hf
        sb = self.stbuf[idx]
        P.dma("sp", lambda hh: hh.dma_start(out=self.EB, in_=self.st_eb[idx].rearrange("p (i t) -> p i t", t=TH)), self.nsem("ld_eb"),
              reads=[sb[2]], writes=list(self.ebbuf) + self.al_eb)
        P.dma("sp", lambda hh: hh.dma_start(out=self.KDEC, in_=self.st_kd[idx].rearrange("p (t d) -> p t d", d=DK)), self.nsem("ld_kd"),
              reads=[sb[0]], writes=self.kdbuf)
        P.dma("sp", lambda hh: hh.dma_start(out=self.VT, in_=self.st_vt[idx].rearrange("p (t d) -> p t d", d=DV)), self.nsem("ld_vt"),
              reads=[sb[1]], writes=self.vtbuf)
        wk = [W.load(("in", C_K + DK * h + 128 * i)) for i in range(2)]
        for i in range(2):
            enb, enbb = self.tf.get()
            P.dma("sp", lambda hh, i=i, enb=enb: hh.dma_start(out=enb[:, :], in_=self.st_enb[idx, i]), self.nsem("ld_enb%d" % i),
                  reads=[sb[3 + i]], writes=[enbb])
            pk, pkb = self.ps("mm")
            self.proj(wk[i], lambda kc: H[:, kc, :], Hb, pk[:, :], pkb)
            self.tt(self.KK[:, i, :], pk[:, :], enb[:, :], ALU.mult, [pkb, enbb], [self.kkbuf[i], self.al_kk[i]])
        wq = [W.load(("in", C_Q + DK * h + 128 * i)) for i in range(2)]
        for i in range(2):
            pq, pqb = self.ps("mm")
            self.proj(wq[i], lambda kc: H[:, kc, :], Hb, pq[:, :], pqb)
            self.stt(self.QQ[:, i, :], pq[:, :], 1.0 / 16.0, self.EB[:, i, :], ALU.mult, ALU.mult,
                     [pqb, self.ebbuf[i]], [self.qqbuf[i], self.al_qq[i]])
        wgo = [W.load(("in", C_GO + DV * h + 128 * i)) for i in range(4)]
        for i in range(4):
            pgo, pgob = self.ps("mm")
            self.proj(wgo[i], lambda kc: H[:, kc, :], Hb, pgo[:, :], pgob)
            self.act(self.SG[:, i, :], pgo[:, :], AF.Silu, [pgob], [self.sgbuf[i], self.al_sg[i]])
        for i in range(2):
            self.act(self.SB16[:, i, :], S[:, h * 2 + i, :], AF.Copy, [Sbuf[h * 2 + i]], [self.sb16buf[i]])

    def gla_head_prep_full(self, W, h, hf, H, Hb, state_only):
        P = self.P
        S = self.S
        Sbuf = self.Sbuf
        wk = [W.load(("in", C_K + DK * h + 128 * i)) for i in range(2)]
        for t in range(4):
            pg, pgb = self.ps("ax")
            P.op("pe", lambda hh, t=t, pg=pg: hh.matmul(pg[:, 0:DK], lhsT=self.GLR[:, t * 128:(t + 1) * 128],
                                                       rhs=self.WGK[:, h * DK:(h + 1) * DK], start=True, stop=True),
                 reads=[self.glrbuf, self.cbuf], writes=[pgb])
            e, eb = self.tf.get()
            self.act(e[:, 0:DK], pg[:, 0:DK], AF.Exp, [pgb], [eb], scale=-1.0)
            self.act(self.GT[:, t, :], e[:, 0:DK], AF.Ln, [eb], [self.gtbuf[t]], bias=self.one_ap[:, 0:1])
        for i in range(2):
            pb, pbb = self.ps("ax")
            for t in range(4):
                P.op("pe", lambda hh, t=t, i=i, pb=pb: hh.matmul(pb[:, t * 128:(t + 1) * 128],
                                                               lhsT=self.GT[:, t, i * 128:(i + 1) * 128],
                                                               rhs=self.CONST[:, 128:256], start=True, stop=True),
                     reads=[self.gtbuf[t], self.cbuf], writes=[pbb])
            self.act(self.EB[:, i, :], pb[:, :], AF.Exp, [pbb], [self.ebbuf[i]])
            if state_only and self.use_stash:
                enb, enbb = self.tf.get()
                self.act(enb[:, :], pb[:, :], AF.Exp, [pbb], [enbb], scale=-1.0)
                idx_ = h * 2 + hf
                P.dma("sp", lambda hh, i=i, enb=enb, idx_=idx_: hh.dma_start(out=self.st_enb[idx_, i], in_=enb[:, :]),
                      self.nsem("st_enb%d" % i), reads=[enbb], writes=[self.stbuf[idx_][3 + i]])
            if not state_only:
                enb, enbb = self.tf.get()
                self.act(enb[:, :], pb[:, :], AF.Exp, [pbb], [enbb], scale=-1.0)
                pk, pkb = self.ps("mm")
                self.proj(wk[i], lambda kc: H[:, kc, :], Hb, pk[:, :], pkb)
                self.tt(self.KK[:, i, :], pk[:, :], enb[:, :], ALU.mult, [pkb, enbb], [self.kkbuf[i]])
        for t in range(4):
            pd, pdb = self.ps("ax")
            P.op("pe", lambda hh, t=t, pd=pd: hh.matmul(pd[:, 0:DK], lhsT=self.CONST[:, 256:384], rhs=self.GT[:, t, :],
                                                       start=True, stop=True),
                 reads=[self.gtbuf[t], self.cbuf], writes=[pdb])
            ed, edb = self.tf.get()
            self.act(ed[:, 0:DK], pd[:, 0:DK], AF.Exp, [pdb], [edb])
            pkt, pktb = self.ps("mm")
            for i in range(2):
                w3, wb = self.w3(wk[i])
                pairs = [(H[:, kc, t * 128:(t + 1) * 128], w3[:, kc, :], [wb, Hb[kc]]) for kc in range(16)]
                self.mm_group(pkt[:, i * 128:(i + 1) * 128], pktb, pairs)
            self.tt(self.KDEC[:, t, :], pkt[:, 0:DK], ed[:, 0:DK], ALU.mult, [pktb, edb], [self.kdbuf[t]])
        if not state_only:
            wq = [W.load(("in", C_Q + DK * h + 128 * i)) for i in range(2)]
            for i in range(2):
                pq, pqb = self.ps("mm")
                self.proj(wq[i], lambda kc: H[:, kc, :], Hb, pq[:, :], pqb)
                self.stt(self.QQ[:, i, :], pq[:, :], 1.0 / 16.0, self.EB[:, i, :], ALU.mult, ALU.mult,
                         [pqb, self.ebbuf[i]], [self.qqbuf[i]])
        wv = [W.load(("in", C_V + DV * h + 128 * i)) for i in range(4)]
        for t in range(4):
            pv, pvb = self.ps("mm")
            for i in range(4):
                w3, wb = self.w3(wv[i])
                pairs = [(H[:, kc, t * 128:(t + 1) * 128], w3[:, kc, :], [wb, Hb[kc]]) for kc in range(16)]
                self.mm_group(pv[:, i * 128:(i + 1) * 128], pvb, pairs)
            self.act(self.VT[:, t, :], pv[:, :], AF.Copy, [pvb], [self.vtbuf[t]])
        if not state_only:
            wgo = [W.load(("in", C_GO + DV * h + 128 * i)) for i in range(4)]
            for i in range(4):
                pgo, pgob = self.ps("mm")
                self.proj(wgo[i], lambda kc: H[:, kc, :], Hb, pgo[:, :], pgob)
                self.act(self.SG[:, i, :], pgo[:, :], AF.Silu, [pgob], [self.sgbuf[i]])
            for i in range(2):
                self.act(self.SB16[:, i, :], S[:, h * 2 + i, :], AF.Copy, [Sbuf[h * 2 + i]], [self.sb16buf[i]])
        if state_only and self.use_stash:
            idx = h * 2 + hf
            sb = self.stbuf[idx]
            P.dma("sp", lambda hh: hh.dma_start(out=self.st_kd[idx].rearrange("p (t d) -> p t d", d=DK), in_=self.KDEC), self.nsem("st_kd"),
                  reads=self.kdbuf, writes=[sb[0]])
            P.dma("sp", lambda hh: hh.dma_start(out=self.st_vt[idx].rearrange("p (t d) -> p t d", d=DV), in_=self.VT), self.nsem("st_vt"),
                  reads=self.vtbuf, writes=[sb[1]])
            P.dma("sp", lambda hh: hh.dma_start(out=self.st_eb[idx].rearrange("p (i t) -> p i t", t=TH), in_=self.EB), self.nsem("st_eb"),
                  reads=self.ebbuf, writes=[sb[2]])

    def gla_head_chunks(self, h, hf, vec, state_only):
        P = self.P
        S = self.S
        Sbuf = self.Sbuf
        for t in range(4):
            tc_ = slice(t * 128, (t + 1) * 128)
            lastc = t * 128 + 127
            if not state_only:
                pss, pssb = self.ps("gl")
                pairs = [(self.KK[:, i, tc_], self.QQ[:, i, tc_], [self.kkbuf[i], self.qqbuf[i]]) for i in range(2)]
                self.mm_group(pss[:, 0:128], pssb, pairs)
                sct, sctb = self.tb.get()
                self.tt(sct[:, 0:128], pss[:, 0:128], self.CONST[:, 0:128], ALU.mult, [pssb, self.cbuf], [sctb])
            pds_l = []
            for i in range(2):
                pds, pdsb = self.ps("mm")
                self.mm_group(pds[:, :], pdsb, [(self.KDEC[:, t, i * 128:(i + 1) * 128], self.VT[:, t, :],
                                                 [self.kdbuf[t], self.vtbuf[t]])])
                pds_l.append((pds, pdsb))
            if not state_only:
                po, pob = self.ps("gl")
                for e in range(4):
                    ec = slice(e * 128, (e + 1) * 128)
                    pairs = [(self.SB16[:, i, ec], self.QQ[:, i, tc_], [self.sb16buf[i], self.qqbuf[i]]) for i in range(2)]
                    pairs.append((self.VT[:, t, ec], sct[:, 0:128], [self.vtbuf[t], sctb]))
                    self.mm_group(po[:, ec], pob, pairs)
                self.act(self.O[:, :, tc_], po[:, :].rearrange("p (e t) -> p e t", t=128), AF.Copy, [pob], [self.obuf] + (self.al_o if self.use_stash else []))
            for i in range(2):
                pds, pdsb = pds_l[i]
                self.stt(S[:, h * 2 + i, :], S[:, h * 2 + i, :], self.EB[:, i, lastc:lastc + 1], pds[:, :],
                         ALU.mult, ALU.add, [Sbuf[h * 2 + i], self.ebbuf[i], pdsb], [Sbuf[h * 2 + i]])
                if state_only:
                    self.tt(self.DACC[:, h * 2 + i:h * 2 + i + 1], self.DACC[:, h * 2 + i:h * 2 + i + 1],
                            self.EB[:, i, lastc:lastc + 1], ALU.mult, [self.daccbuf, self.ebbuf[i]], [self.daccbuf])
                elif t < 3:
                    self.act(self.SB16[:, i, :], S[:, h * 2 + i, :], AF.Copy, [Sbuf[h * 2 + i]], [self.sb16buf[i]])
        if state_only:
            return
        pn, pnb = self.ps("ax")
        for e in range(4):
            sq, sqb = self.tb.get()
            self.act(sq[:, :], self.O[:, e, :], AF.Square, [self.obuf], [sqb])
            P.op("pe", lambda hh, e=e, sq=sq: hh.matmul(pn[:, :], lhsT=self.ones_dv[:], rhs=sq[:, :], start=(e == 0), stop=(e == 3)),
                 reads=[sqb, self.cbuf], writes=[pnb])
        rs = self.RS
        rsbufs = [self.gtbuf[0], self.gtbuf[1]]
        self.act(rs, pn[:, :], AF.Ln, [pnb], rsbufs + (self.al_rs if self.use_stash else []), bias=self.eps_ap[:, 0:1])
        self.act(rs, rs, AF.Exp, rsbufs, rsbufs, scale=-0.5)
        for e in range(4):
            t1, t1b = self.tf.get()
            self.tt(t1[:, :], self.O[:, e, :], rs, ALU.mult, [self.obuf] + rsbufs, [t1b])
            self.stt(self.G[:, h * 4 + e, :], t1[:, :], vec[:, V_GNORM + e:V_GNORM + e + 1], self.SG[:, e, :],
                     ALU.mult, ALU.mult, [t1b, self.sgbuf[e], self.vbuf],
                     [self.gbuf] + ([self.ucfbuf[(h * 4 + e) // 2]] if self.use_stash else []))

    def glr_proj(self, W, H, Hb):
        wg = W.load(("in", C_GK))
        pg, pgb = self.ps("ax")
        self.proj(wg, lambda kc: H[:, kc, :], Hb, pg[0:16, :], pgb, lcols=slice(0, 16))
        self.act(self.GLR[0:16, :], pg[0:16, :], AF.Copy, [pgb], [self.glrbuf])

    def mixer_half(self, W, hf, vec, state_cb=None):
        P = self.P
        X, XB = self.X, self.XB
        H, Hb = self.H, self.Hbuf
        c0 = hf * TH
        cs = slice(c0, c0 + TH)
        self.rmsnorm([X[:, c, cs] for c in range(NCH)], [XB[c][hf] for c in range(NCH)], V_GMIX,
                     [H[:, c, :] for c in range(NCH)], Hb, TH, vec)
        for j in range(8):
            wc = W.load(("in", C_SCC + 128 * j))
            wx = W.load(("in", C_SCX + 128 * j))
            wb_ = W.load(("in", C_SCB + 128 * j))
            ub, ubb = self.ubp.get()
            pc, pcb = self.ps("mm")
            self.proj(wc, lambda kc: H[:, kc, :], Hb, pc[:, :], pcb)
            px, pxb = self.ps("mm")
            self.proj(wx, lambda kc: H[:, kc, :], Hb, px[:, :], pxb)
            if hf == 0:
                ph, phb = self.ps("gl")
                self.proj(wc, lambda kc: self.HH[:, kc, :], [self.hhbuf], ph[:, 0:HALO], phb)
                self.proj(wx, lambda kc: self.HH[:, kc, :], [self.hhbuf], ph[:, 64:64 + HALO], phb)
                th, thb = self.tf.get()
                self.act(th[:, 0:HALO], ph[:, 0:HALO], AF.Copy, [phb], [thb])
                self.tt(ub[:, 0:HALO], th[:, 0:HALO], ph[:, 64:64 + HALO], ALU.mult, [thb, phb], [ubb])
            else:
                self.act(ub[:, 0:HALO], self.HALOB[:, j, :], AF.Copy, [self.halobuf[j]], [ubb])
            tcx, tcxb = self.tf.get()
            self.act(tcx[:, :], pc[:, :], AF.Copy, [pcb], [tcxb])
            self.tt(ub[:, HALO:HALO + TH], tcx[:, :], px[:, :], ALU.mult, [tcxb, pxb], [ubb])
            if hf == 0:
                self.act(self.HALOB[:, j, :], ub[:, TH:TH + HALO], AF.Copy, [ubb], [self.halobuf[j]])
            acc, accb = self.tf.get()
            self.ts(acc[:, :], ub[:, HALO - 2:HALO - 2 + TH], vec[:, V_SCW + j:V_SCW + j + 1], vec[:, V_SCB + j:V_SCB + j + 1],
                    ALU.mult, ALU.add, [ubb, self.vbuf], [accb])
            for k in (1, 2):
                self.stt(acc[:, :], ub[:, HALO - 2 + k:HALO - 2 + k + TH], vec[:, V_SCW + k * 8 + j:V_SCW + k * 8 + j + 1],
                         acc[:, :], ALU.mult, ALU.add, [ubb, accb, self.vbuf], [accb])
            pb, pbb = self.ps("mm")
            self.proj(wb_, lambda kc: H[:, kc, :], Hb, pb[:, :], pbb)
            self.tt(self.A[:, j, :], acc[:, :], pb[:, :], ALU.mult, [accb, pbb], [self.abuf])
        pmu, pmub = self.ps("ax")
        psq, psqb = self.ps("ax")
        ubs = {}

        def stage_a(j):
            wg = W.load(("in", C_CFG + 128 * j))
            wa = W.load(("in", C_CFA + 128 * j))
            ub, ubb = self.ub16p.get()
            ubs[j] = (ub, ubb)
            pg, pgb = self.ps("mm")
            self.proj(wg, lambda kc: H[:, kc, :], Hb, pg[:, :], pgb)
            pa, pab = self.ps("mm")
            self.proj(wa, lambda kc: H[:, kc, :], Hb, pa[:, :], pab)
            if hf == 0:
                ph, phb = self.ps("gl")
                self.proj(wg, lambda kc: self.HH[:, kc, :], [self.hhbuf], ph[:, 0:HALO], phb)
                self.proj(wa, lambda kc: self.HH[:, kc, :], [self.hhbuf], ph[:, 64:64 + HALO], phb)
                th, thb = self.tf.get()
                self.act(th[:, 0:HALO], ph[:, 0:HALO], AF.Sigmoid, [phb], [thb])
                self.tt(ub[:, 0:HALO], th[:, 0:HALO], ph[:, 64:64 + HALO], ALU.mult, [thb, phb], [ubb])
            else:
                self.act(ub[:, 0:HALO], self.HALOB[:, 8 + j, :], AF.Copy, [self.halobuf[8 + j]], [ubb])
            tg, tgb = self.tf.get()
            self.act(tg[:, :], pg[:, :], AF.Sigmoid, [pgb], [tgb])
            self.tt(ub[:, HALO:HALO + TH], tg[:, :], pa[:, :], ALU.mult, [tgb, pab], [ubb])
            if hf == 0:
                self.act(self.HALOB[:, 8 + j, :], ub[:, TH:TH + HALO], AF.Copy, [ubb], [self.halobuf[8 + j]])

        def stage_b(j):
            ub, ubb = ubs.pop(j)
            pcv, pcvb = self.ps("gl")
            for k in range(31):
                dg, dgb = self.diagp.get()
                self.ts(dg, self.IDB[:, :], vec[:, V_CFW + k * 8 + j:V_CFW + k * 8 + j + 1], None, ALU.mult, None,
                        [self.cbuf, self.vbuf], [dgb])
                P.op("pe", lambda hh, k=k, dg=dg, ub=ub, pcv=pcv: hh.matmul(pcv[:, :], lhsT=dg, rhs=ub[:, 2 + k:2 + k + TH],
                                                                           start=(k == 0), stop=(k == 30)),
                     reads=[dgb, ubb], writes=[pcvb])
            U = self.UCF[:, j, :]
            Ub = self.ucfbuf[j]
            self.act(U, pcv[:, :], AF.Identity, [pcvb, self.vbuf], [Ub], bias=vec[:, V_CFB + j:V_CFB + j + 1])
            u16, u16b = self.tb.get()
            self.act(u16[:, :], U, AF.Copy, [Ub], [u16b])
            usq, usqb = self.tb.get()
            self.act(usq[:, :], U, AF.Square, [Ub], [usqb])
            return (u16, u16b, usq, usqb)

        def stage_c(j, st):
            u16, u16b, usq, usqb = st
            P.op("pe", lambda hh: hh.matmul(pmu[:, :], lhsT=self.ones_cf[:], rhs=u16[:, :], start=(j == 0), stop=(j == 7)),
                 reads=[u16b, self.cbuf], writes=[pmub])
            P.op("pe", lambda hh: hh.matmul(psq[:, :], lhsT=self.ones_cf[:], rhs=usq[:, :], start=(j == 0), stop=(j == 7)),
                 reads=[usqb, self.cbuf], writes=[psqb])

        stats = {}
        stage_a(0)
        for j in range(8):
            if j + 1 < 8:
                stage_a(j + 1)
            stats[j] = stage_b(j)
            if state_cb is not None:
                state_cb(j)
            if j >= 1:
                stage_c(j - 1, stats.pop(j - 1))
        stage_c(7, stats.pop(7))
        mu, mub = self.MU, self.mubuf
        var, varb = self.VAR, self.varbuf
        self.act(mu, pmu[:, :], AF.Copy, [pmub], [mub])
        self.tt(var, mu, mu, ALU.mult, [mub], [varb])
        self.tt(var, psq[:, :], var, ALU.subtract, [psqb, varb], [varb])
        self.ts(var, var, 0.0, None, ALU.max, None, [varb], [varb])
        self.act(var, var, AF.Ln, [varb], [varb], bias=self.eps_ap[:, 0:1])
        self.act(var, var, AF.Exp, [varb], [varb], scale=-0.5)
        self.stt(mu, mu, -1.0, var, ALU.mult, ALU.mult, [mub, varb], [mub])
        for j in range(8):
            t1, t1b = self.tf.get()
            self.tt(t1[:, :], self.UCF[:, j, :], var, ALU.mult, [self.ucfbuf[j], varb], [t1b])
            self.tt(t1[:, :], t1[:, :], mu, ALU.add, [t1b, mub], [t1b])
            self.act(self.B[:, j, :], t1[:, :], AF.Silu, [t1b, self.vbuf], [self.bbuf],
                     scale=vec[:, V_LNG + j:V_LNG + j + 1], bias=vec[:, V_LNB + j:V_LNB + j + 1])
        if not self.use_stash:
            P.barrier()
        if not self.use_stash:
            self.glr_proj(W, H, Hb)
        for h in range(NH):
            self.gla_head(W, h, hf, H, Hb, vec, state_only=False)
        if not self.use_stash:
            P.barrier()
        for n in range(NCH):
            acc, accb = self.tf.get()
            wab = W.load(("outab", n))
            for bi in range(3):
                wmt = W.load(("in", (C_MA, C_MB, C_MC)[bi] + 128 * n))
                if bi == 2:
                    wgl = W.load(("glaout", n))
                wt, kc0, nkc, SRC, srcb = [(wab, 0, 8, self.A, self.abuf), (wab, 8, 8, self.B, self.bbuf),
                                            (wgl if bi == 2 else None, 0, 16, self.G, self.gbuf)][bi]
                pu, pub = self.ps("mm")
                self.proj(wt, lambda kc, SRC=SRC: SRC[:, kc, :], [srcb], pu[:, :], pub, kc0=kc0, nkc=nkc)
                pm, pmb = self.ps("mm")
                self.proj(wmt, lambda kc: H[:, kc, :], Hb, pm[:, :], pmb)
                sg, sgb = self.tf.get()
                self.act(sg[:, :], pm[:, :], AF.Sigmoid, [pmb], [sgb])
                if bi == 0:
                    self.tt(acc[:, :], sg[:, :], pu[:, :], ALU.mult, [sgb, pub], [accb])
                else:
                    self.tt(sg[:, :], sg[:, :], pu[:, :], ALU.mult, [sgb, pub], [sgb])
                    if bi == 1:
                        self.tt(acc[:, :], acc[:, :], sg[:, :], ALU.add, [accb, sgb], [accb])
                    else:
                        mg_alias = ([self.obuf] if n < 8 else [self.sgbuf[n - 8]] if n < 12 else
                                    [self.kkbuf[n - 12]] if n < 14 else [self.qqbuf[n - 14]]) if self.use_stash else []
                        self.tt(self.MG[:, n, :], acc[:, :], sg[:, :], ALU.add, [accb, sgb], [self.mgbuf] + mg_alias)
        for n in range(NCH):
            wo = W.load(("wo", n))
            po, pob = self.ps("mm")
            self.proj(wo, lambda kc: self.MG[:, kc, :], [self.mgbuf], po[:, :], pob)
            self.tt(X[:, n, cs], X[:, n, cs], po[:, :], ALU.add, [XB[n][hf], pob], [XB[n][hf]])
        P.barrier()

    def ffn_ple(self, W, vec, pT):
        P = self.P
        X, XB = self.X, self.XB
        H2 = self.H2
        for hf in range(2):
            cs = slice(hf * TH, (hf + 1) * TH)
            self.rmsnorm([X[:, c, cs] for c in range(NCH)], [XB[c][hf] for c in range(NCH)], V_GFFN,
                         [H2[:, c, cs] for c in range(NCH)], self.h2buf[hf], TH, vec)
        for fg in range(NFG):
            HID = self.HID
            hidb = self.hidbuf
            for f in range(FGS):
                wg = W.load(("gate", fg * FGS + f))
                wu = W.load(("up", fg * FGS + f))
                for hf in range(2):
                    cs = slice(hf * TH, (hf + 1) * TH)
                    pg, pgb = self.ps("mm")
                    self.proj(wg, lambda kc, cs=cs: H2[:, kc, cs], self.h2buf[hf], pg[:, :], pgb)
                    pu, pub = self.ps("mm")
                    self.proj(wu, lambda kc, cs=cs: H2[:, kc, cs], self.h2buf[hf], pu[:, :], pub)
                    sg, sgb = self.tf.get()
                    self.act(sg[:, :], pg[:, :], AF.Silu, [pgb], [sgb])
                    self.tt(HID[:, f, cs], sg[:, :], pu[:, :], ALU.mult, [sgb, pub], [hidb[hf]])
            for n in range(NCH):
                wd = W.load(("down", fg, n), ncols=FGS * 128)
                w3, wb = self.w3(wd, kc=FGS)
                for hf in range(2):
                    cs = slice(hf * TH, (hf + 1) * TH)
                    pd, pdb = self.ps("mm")
                    pairs = [(w3[:, f, :], HID[:, f, cs], [wb, hidb[hf]]) for f in range(FGS)]
                    self.mm_group(pd[:, :], pdb, pairs)
                    self.tt(X[:, n, cs], X[:, n, cs], pd[:, :], ALU.add, [XB[n][hf], pdb], [XB[n][hf]])
        for hf in range(2):
            cs = slice(hf * TH, (hf + 1) * TH)
            self.rmsnorm([X[:, c, cs] for c in range(NCH)], [XB[c][hf] for c in range(NCH)], V_GPLE,
                         [H2[:, c, cs] for c in range(NCH)], self.h2buf[hf], TH, vec)
        hb2 = [self.hidbuf[0], self.hidbuf[1]]
        for kc in range(2):
            P.dma("pool", lambda h, kc=kc: h.dma_start(out=self.PT[:, kc, :], in_=pT[kc * 128:(kc + 1) * 128, :]), self.pt_sem,
                  writes=hb2)
        for g in range(2):
            idx = W.index[("ple", g)]
            dram = W.dram
            P.dma("pool", lambda h, g=g, idx=idx: h.dma_start(out=self.PLEW[:, g * 2048:(g + 1) * 2048], in_=dram[idx, :, :]),
                  self.plew_sem, writes=hb2)
        plew = self.PLEW.rearrange("p (n k j) -> p n k j", k=2, j=128)
        for n in range(NCH):
            wpg = W.load(("plegate", n))
            for hf in range(2):
                cs = slice(hf * TH, (hf + 1) * TH)
                pg, pgb = self.ps("mm")
                self.proj(wpg, lambda kc, cs=cs: H2[:, kc, cs], self.h2buf[hf], pg[:, :], pgb)
                pp, ppb = self.ps("mm")
                pairs = [(plew[:, n, kc, :], self.PT[:, kc, cs], hb2) for kc in range(2)]
                self.mm_group(pp[:, :], ppb, pairs)
                sg, sgb = self.tf.get()
                self.act(sg[:, :], pg[:, :], AF.Sigmoid, [pgb], [sgb])
                self.tt(sg[:, :], sg[:, :], pp[:, :], ALU.mult, [sgb, ppb], [sgb])
                self.tt(X[:, n, cs], X[:, n, cs], sg[:, :], ALU.add, [XB[n][hf], sgb], [XB[n][hf]])

    def prepass(self, W, vec, wgk_dram, mid_cb=None):
        P = self.P
        X, XB = self.X, self.XB
        H, Hb = self.H, self.Hbuf
        P.dma("pool", lambda h: h.dma_start(out=self.WGK[:, :], in_=wgk_dram[:, :]), self.nsem("wgkp"), writes=[self.cbuf])
        P.op("dve", lambda h: h.memset(self.S[:, :, :], 0.0), writes=self.Sbuf)
        P.op("dve", lambda h: h.memset(self.DACC[:, :], 1.0), writes=[self.daccbuf])
        for hf in range(2):
            cs = slice(hf * TH, (hf + 1) * TH)
            self.rmsnorm([X[:, c, cs] for c in range(NCH)], [XB[c][hf] for c in range(NCH)], V_GMIX,
                         [H[:, c, :] for c in range(NCH)], Hb, TH, vec)
            self.glr_proj(W, H, Hb)
            for h in range(NH):
                self.gla_head(W, h, hf, H, Hb, vec, state_only=True)
            if hf == 0 and mid_cb is not None:
                mid_cb()
        P.barrier()

    def halo_norm(self, vec):
        self.rmsnorm([self.XH[:, c, :] for c in range(NCH)], [self.xhbuf] * NCH, V_GMIX,
                     [self.HH[:, c, :] for c in range(NCH)], [self.hhbuf] * NCH, HALO, vec)

    def misc_sem(self):
        return self.P.new_dma_sem()

    def nsem(self, name):
        if name not in self._nsems:
            self._nsems[name] = self.P.new_dma_sem()
        return self._nsems[name]

    def build(self):
        mode = self.mode
        nc = self.nc
        fused = mode == "fused"
        self.use_stash = fused
        has_main = mode in ("mainpre", "mainfinal", "fused")
        has_pre = mode in ("pre", "mainpre")
        has_final = mode in ("mainfinal", "fused")
        mkeys = main_keys()
        pkeys = pre_keys()
        PK = 8 * DV + NCH * HALO + 8
        xT = self.dram_in("xT", [D, T])
        const_d = self.dram_in("const", [128, 512])
        if fused:
            pT4 = self.dram_in("pT4", [DEPTH, PLE, T])
            was = [self.dram_in("wa%d" % l, [len(mkeys), 128, 2048]) for l in range(DEPTH)]
            vec4 = self.dram_in("vec4", [DEPTH, 128, NV])
            wgk4 = self.dram_in("wgk4", [DEPTH, 32, 1024])
            rmask = self.dram_in("rmask", [128, 9])
            self.st_kd = nc.dram_tensor("st_kd", [8, 128, 4 * DK], BF16).ap()
            self.st_vt = nc.dram_tensor("st_vt", [8, 128, 4 * DV], BF16).ap()
            self.st_eb = nc.dram_tensor("st_eb", [8, 128, 2 * TH], F32).ap()
            self.st_enb = nc.dram_tensor("st_enb", [8, 2, 128, TH], F32).ap()
            self.stbuf = [[Buf("st%d_%d" % (a, k)) for k in range(5)] for a in range(8)]
            XC = (2 * DV, 2 * DV, 2 * DV, 2 * DV, NCH * HALO, 8)
            xsrc = [[nc.dram_tensor("xsrc%d_%d" % (i, k), [128, XC[k]], F32).ap() for k in range(6)] for i in range(2)]
            xdst = [[nc.dram_tensor("xdst%d_%d" % (i, k), [4 * 128, XC[k]], F32).ap() for k in range(6)] for i in range(2)]
        elif has_main:
            xh = self.dram_in("xh", [D, HALO])
            pT = self.dram_in("pT", [PLE, T])
            sall = self.dram_in("sall", [3, 8, 128, DV])
            dall = self.dram_in("dall", [128, 24])
            rmask = self.dram_in("rmask", [128, 9])
            wa = self.dram_in("wa", [len(mkeys), 128, 2048])
            vec_d = self.dram_in("vec", [128, NV])
            wgk_d = self.dram_in("wgk", [32, 1024])
        if has_pre:
            wp = self.dram_in("wp", [len(pkeys), 128, 2048])
            vecp_d = self.dram_in("vecp", [128, 16])
            wgkp_d = self.dram_in("wgkp", [32, 1024])
            sloc = self.dram_out("sloc", [8, 128, DV])
            dloc = self.dram_out("dloc", [128, 8])
        if mode == "mainpre":
            xTo = self.dram_out("xTo", [D, T])
        if has_final:
            outT = self.dram_out("outT", [D, T])

        with ExitStack() as es:
            self.es = es
            P = self.P = Prog(nc, es)
            self.X = self.sb("X", [128, NCH, T], F32)
            self.XB = [[Buf("X%d_%d" % (c, hf)) for hf in range(2)] for c in range(NCH)]
            self.VEC = self.sb("VEC", [128, NV], F32)
            if has_pre:
                self.VECP = self.sb("VECP", [128, 16], F32)
            self.vbuf = Buf("vec")
            self.CONST = self.sb("CONST", [128, 512], F32)
            self.IDB = self.sb("IDB", [128, 128], BF16)
            self.WGK = self.sb("WGK", [32, 1024], BF16)
            self.GLR = self.sb("GLR", [32, TH], BF16)
            self.cbuf = Buf("const")
            self.glrbuf = Buf("glr")
            self.ones_d = self.sb("ones_d", [128, 128], BF16)
            self.ones_cf = self.sb("ones_cf", [128, 128], BF16)
            self.ones_dv = self.sb("ones_dv", [128, 128], BF16)
            self.eps_ap = self.sb("eps", [128, 1], F32)
            self.one_ap = self.sb("one", [128, 1], F32)
            self.S = self.sb("S", [128, 8, DV], F32)
            self.Sbuf = [Buf("S%d" % a) for a in range(8)]
            self.DACC = self.sb("DACC", [128, 8], F32)
            self.daccbuf = Buf("dacc")
            self.RMASK = self.sb("RMASK", [128, 9], F32)
            self.DALL = self.sb("DALL", [128, 24], F32)
            self.DPALL = self.sb("DPALL", [128, 24], F32)
            self.dpbuf = Buf("dpall")
            self.smallbuf = Buf("small")
            self.rmbuf = Buf("rmask")
            self.HALOB = self.sb("HALOB", [128, 16, HALO], F32)
            self.halobuf = [Buf("halo%d" % j) for j in range(16)]
            self.ring = [self.sb("ring%d" % s_, [128, 2048], BF16) for s_ in range(NSLOT)]
            self.ring_buf = [Buf("ring%d" % s_) for s_ in range(NSLOT)]
            self.ring_sem = [P.new_dma_sem() for s_ in range(NSLOT)]
            self.ring_gen = [0] * NSLOT
            self.ring_i = 0
            tfn, tbn = 5, 4
            self.tf = Pool([self.sb("tf%d" % i, [128, TH], F32) for i in range(tfn)])
            self.tb = Pool([self.sb("tb%d" % i, [128, TH], BF16) for i in range(tbn)])
            banks = [es.enter_context(nc.psum_tensor("ps%d" % i, [128, 512], F32)) for i in range(8)]
            self.pspool = {"mm": Pool(banks[0:4]), "ax": Pool(banks[4:6]), "gl": Pool(banks[6:8])}
            AW = 20480 if has_main else 7680
            self.arena = self.sb("arena", [128, AW], F32)

            def af(off, n):
                assert off + n <= AW
                return self.arena[:, off:off + n]

            def ab(off, n):
                return af(off, n).bitcast(BF16)

            self.H = ab(0, 4096).rearrange("p (c t) -> p c t", t=TH)
            self.Hbuf = KcBufs(Buf("H%d" % c) for c in range(NCH))
            self.GT = af(4096, 1024).rearrange("p (t d) -> p t d", d=DK)
            self.gtbuf = [Buf("gt%d" % t) for t in range(4)]
            self.EB = af(5120, 1024).rearrange("p (i t) -> p i t", t=TH)
            self.ebbuf = [Buf("eb%d" % i) for i in range(2)]
            self.KDEC = ab(6144, 512).rearrange("p (t d) -> p t d", d=DK)
            self.kdbuf = [Buf("kd%d" % t) for t in range(4)]
            self.VT = ab(6656, 1024).rearrange("p (t d) -> p t d", d=DV)
            self.vtbuf = [Buf("vt%d" % t) for t in range(4)]
            self.RS = af(4096, 512)
            if has_main:
                self.ubp = Pool([af(4096, HALO + TH), af(4096 + 544, HALO + TH)])
                self.ub16p = Pool([ab(4096 + 1088, 272), ab(4096 + 1360, 272)])
                self.diagp = Pool([ab(4096 + 1632 + 64 * i, 64) for i in range(6)])
                M0 = 7680
                self.SB16 = ab(M0, 512).rearrange("p (i t) -> p i t", t=DV)
                self.sb16buf = [Buf("sb16_%d" % i) for i in range(2)]
                self.A = ab(M0 + 512, 2048).rearrange("p (c t) -> p c t", t=TH)
                self.abuf = Buf("A")
                self.B = ab(M0 + 2560, 2048).rearrange("p (c t) -> p c t", t=TH)
                self.bbuf = Buf("B")
                R1 = M0 + 4608
                self.UCF = af(R1, 4096).rearrange("p (c t) -> p c t", t=TH)
                self.ucfbuf = [Buf("ucf%d" % j) for j in range(8)]
                self.G = ab(R1, 4096).rearrange("p (c t) -> p c t", t=TH)
                self.gbuf = Buf("G")
                R2 = R1 + 4096
                self.XHF = af(R2, 512)
                self.XH = af(R2, 512).rearrange("p (c t) -> p c t", t=HALO)
                self.xhbuf = Buf("XH")
                self.HH = ab(R2 + 512, 256).rearrange("p (c t) -> p c t", t=HALO)
                self.hhbuf = Buf("HH")
                self.MU = af(R2 + 1024, 512)
                self.VAR = af(R2 + 1536, 512)
                self.mubuf = Buf("mu")
                self.varbuf = Buf("var")
                self.O = af(R2, 2048).rearrange("p (e t) -> p e t", t=TH)
                self.obuf = Buf("O")
                self.SG = ab(R2 + 2048, 1024).rearrange("p (i t) -> p i t", t=TH)
                self.sgbuf = [Buf("sg%d" % i) for i in range(4)]
                self.KK = ab(R2 + 3072, 512).rearrange("p (i t) -> p i t", t=TH)
                self.kkbuf = [Buf("kk%d" % i) for i in range(2)]
                self.QQ = ab(R2 + 3584, 512).rearrange("p (i t) -> p i t", t=TH)
                self.qqbuf = [Buf("qq%d" % i) for i in range(2)]
                self.MG = ab(R2, 4096).rearrange("p (c t) -> p c t", t=TH)
                self.mgbuf = Buf("MG")
                assert R2 + 4096 == AW
                self.H2 = ab(0, 8192).rearrange("p (c t) -> p c t", t=T)
                self.h2buf = [KcBufs(Buf("h2_%d_%d" % (hf, c)) for c in range(NCH)) for hf in range(2)]
                self.HID = ab(8192, 5632).rearrange("p (f t) -> p f t", t=T)
                self.hidbuf = [Buf("hid_%d" % hf) for hf in range(2)]
                self.PT = ab(8192, 1024).rearrange("p (k t) -> p k t", t=T)
                self.PLEW = ab(8192 + 1024, 2048)
                self.plew_sem = P.new_dma_sem()
                self.pt_sem = P.new_dma_sem()
                self.stgp = Pool([af(R2 + 2048 + 512 * i, 512) for i in range(4)])
                self.al_eb = [self.ubp.bufs[1], self.ub16p.bufs[0], self.ub16p.bufs[1]] + list(self.diagp.bufs)
                self.al_rs = [self.ubp.bufs[0]]
                self.al_sg = [self.stgp.bufs[0], self.stgp.bufs[0], self.stgp.bufs[1], self.stgp.bufs[1]]
                self.al_kk = [self.stgp.bufs[2], self.stgp.bufs[2]]
                self.al_qq = [self.stgp.bufs[3], self.stgp.bufs[3]]
                self.al_o = [self.xhbuf, self.hhbuf, self.mubuf, self.varbuf]
                self.FST = [af(512 * i, 512) for i in range(16)]

            s0 = P.new_dma_sem()
            P.dma("sp", lambda h: h.dma_start(out=self.CONST[:, :], in_=const_d[:, :]), s0, writes=[self.cbuf])
            P.op("dve", lambda h: h.memset(self.ones_d[:, :], 1.0 / D), writes=[self.cbuf])
            P.op("dve", lambda h: h.memset(self.ones_cf[:, :], 1.0 / 1024.0), writes=[self.cbuf])
            P.op("dve", lambda h: h.memset(self.ones_dv[:, :], 1.0 / DV), writes=[self.cbuf])
            P.op("dve", lambda h: h.memset(self.eps_ap[:, :], EPS), writes=[self.cbuf])
            P.op("dve", lambda h: h.memset(self.one_ap[:, :], 1.0), writes=[self.cbuf])
            P.op("dve", lambda h: h.memset(self.GLR[:, :], 1.0), writes=[self.glrbuf])
            P.op("dve", lambda h: h.tensor_copy(out=self.IDB[:, :], in_=self.CONST[:, 384:512]), reads=[self.cbuf], writes=[self.cbuf])
            xsem = [P.new_dma_sem() for _ in range(8)]
            xTv = xT.rearrange("(c p) t -> p c t", p=128)
            for hf_ in range(2):
                for g in range(4):
                    P.dma("sp", lambda h, g=g, hf_=hf_: h.dma_start(out=self.X[:, 4 * g:4 * g + 4, hf_ * TH:(hf_ + 1) * TH],
                                                                   in_=xTv[:, 4 * g:4 * g + 4, hf_ * TH:(hf_ + 1) * TH]),
                          xsem[hf_ * 4 + g], writes=[self.XB[c][hf_] for c in range(4 * g, 4 * g + 4)])
            self.out_toks = []

            def build_xh(xh_srcs, src_bufs=()):
                if xh_srcs is None:
                    P.dma("sp", lambda h: h.dma_start(out=self.XH, in_=xh.rearrange("(c p) t -> p c t", p=128)), self.nsem("xh"),
                          writes=[self.xhbuf])
                    return
                P.op("dve", lambda h: h.memset(self.XHF, 0.0), writes=[self.xhbuf])
                for r in range(3):
                    t1, t1b = self.tf.get()
                    P.dma("sp", lambda h, r=r, t1=t1: h.dma_start(out=t1[:, :], in_=xh_srcs(r)), self.nsem("xhr%d" % r),
                          reads=list(src_bufs), writes=[t1b])
                    self.stt(self.XHF, t1[:, :], self.RMASK[:, 6 + r:7 + r], self.XHF, ALU.mult, ALU.add,
                             [t1b, self.rmbuf, self.xhbuf], [self.xhbuf])

            def combine_begin(dall_src, src_bufs=()):
                for r in range(3):
                    P.dma("sp", lambda h, r=r: h.dma_start(out=self.DALL[:, r * 8:(r + 1) * 8], in_=dall_src(r)), self.nsem("dall"),
                          reads=list(src_bufs), writes=[self.smallbuf])

            def combine_tile(a, stg_src, src_bufs=()):
                if a == 0:
                    P.op("dve", lambda h: h.memset(self.S[:, :, :], 0.0), writes=self.Sbuf)
                    for r in range(3):
                        self.ts(self.DPALL[:, r * 8:(r + 1) * 8], self.DALL[:, r * 8:(r + 1) * 8], self.RMASK[:, r:r + 1],
                                self.RMASK[:, 3 + r:4 + r], ALU.mult, ALU.add, [self.smallbuf, self.rmbuf], [self.dpbuf])
                for r in range(3):
                    st, stb = self.stgp.get()
                    P.dma("sp", lambda h, r=r, st=st: h.dma_start(out=st, in_=stg_src(r, a)), self.nsem("stg%d" % ((self.stgp.i - 1) % 4)),
                          reads=list(src_bufs), writes=[stb])
                    self.ts(st, st, self.RMASK[:, r:r + 1], None, ALU.mult, None, [stb, self.rmbuf], [stb])
                    self.stt(self.S[:, a, :], self.S[:, a, :], self.DPALL[:, r * 8 + a:r * 8 + a + 1], st, ALU.mult, ALU.add,
                             [self.Sbuf[a], self.dpbuf, stb], [self.Sbuf[a]])

            if has_main:
                P.dma("sp", lambda h: h.dma_start(out=self.RMASK[:, :], in_=rmask[:, :]), self.nsem("rmask"), writes=[self.rmbuf])

            if has_main and not fused:
                P.dma("sp", lambda h: h.dma_start(out=self.VEC[:, :], in_=vec_d[:, :]), self.nsem("vec"), writes=[self.vbuf])
                P.dma("pool", lambda h: h.dma_start(out=self.WGK[:, :], in_=wgk_d[:, :]), self.nsem("wgkp"), writes=[self.cbuf])
                build_xh(None)
                self.halo_norm(self.VEC)
                W = WStream(self, wa, mkeys)
                combine_begin(lambda r: dall[:, r * 8:(r + 1) * 8])
                self.mixer_half(W, 0, self.VEC, state_cb=lambda a: combine_tile(a, lambda r, a: sall[r, a]))
                self.mixer_half(W, 1, self.VEC)
                self.ffn_ple(W, self.VEC, pT)
                P.barrier()

            if fused:
                xs_bufs = [[Buf("xsrc%d_%d" % (i, k)) for k in range(6)] for i in range(2)]
                xd_buf = [[Buf("xdst%d_%d" % (i, k)) for k in range(6)] for i in range(2)]
                ccsem = P.new_dma_sem()
                GRP = [[0, 1, 2, 3], [4, 5, 6, 7]]

                def allgather(i, k):
                    P.dma("pool", lambda h, i=i, k=k: h.collective_compute("AllGather", ALU.bypass, replica_groups=GRP,
                                                                           ins=[xsrc[i][k][:, :]], outs=[xdst[i][k][:, :]]),
                          ccsem, reads=[xs_bufs[i][k]], writes=[xd_buf[i][k]], inc=1)

                for l in range(DEPTH):
                    i = l % 2
                    W = WStream(self, was[l], mkeys)
                    P.dma("sp", lambda h, l=l: h.dma_start(out=self.VEC[:, :], in_=vec4[l]), self.nsem("vec"), writes=[self.vbuf])
                    P.dma("sp", lambda h, i=i: h.dma_start(out=xsrc[i][4][:, :].rearrange("p (c t) -> p c t", t=HALO),
                                                          in_=self.X[:, :, T - HALO:T]),
                          self.nsem("xsH"), reads=[self.XB[c][1] for c in range(NCH)], writes=[xs_bufs[i][4]])
                    allgather(i, 4)
                    def mid(i=i):
                        build_xh(lambda r: xdst[i][4][r * 128:(r + 1) * 128, :], src_bufs=[xd_buf[i][4]])
                        self.halo_norm(self.VEC)
                    self.prepass(W, self.VEC, wgk4[l], mid_cb=mid)
                    for k in range(4):
                        P.dma("sp", lambda h, i=i, k=k: h.dma_start(out=xsrc[i][k][:, :].rearrange("p (a e) -> p a e", e=DV),
                                                                   in_=self.S[:, 2 * k:2 * k + 2, :]),
                              self.nsem("xsS%d" % k), reads=self.Sbuf[2 * k:2 * k + 2], writes=[xs_bufs[i][k]])
                    P.dma("sp", lambda h, i=i: h.dma_start(out=xsrc[i][5][:, :], in_=self.DACC[:, :]),
                          self.nsem("xsD"), reads=[self.daccbuf], writes=[xs_bufs[i][5]])
                    allgather(i, 5)
                    for k in range(4):
                        allgather(i, k)
                    combine_begin(lambda r, i=i: xdst[i][5][r * 128:(r + 1) * 128, :], src_bufs=[xd_buf[i][5]])
                    self.mixer_half(W, 0, self.VEC, state_cb=lambda a, i=i: combine_tile(
                        a, lambda r, a: xdst[i][a // 2][r * 128:(r + 1) * 128, (a % 2) * DV:(a % 2 + 1) * DV],
                        src_bufs=[xd_buf[i][a // 2]]))
                    self.mixer_half(W, 1, self.VEC)
                    self.ffn_ple(W, self.VEC, pT4[l])
                    P.barrier()

            if mode == "mainpre":
                so = [P.new_dma_sem() for _ in range(4)]
                xTov = xTo.rearrange("(c p) t -> p c t", p=128)
                for g in range(4):
                    tk = P.dma("sp", lambda h, g=g: h.dma_start(out=xTov[:, 4 * g:4 * g + 4, :], in_=self.X[:, 4 * g:4 * g + 4, :]), so[g],
                               reads=[self.XB[c][hf] for c in range(4 * g, 4 * g + 4) for hf in range(2)])
                    self.out_toks.append(tk)

            if has_pre:
                s8 = P.new_dma_sem()
                P.dma("sp", lambda h: h.dma_start(out=self.VECP[:, :], in_=vecp_d[:, :]), s8, writes=[self.vbuf])
                Wp = WStream(self, wp, pkeys)
                self.prepass(Wp, self.VECP, wgkp_d)
                s9 = P.new_dma_sem()
                tk = P.dma("sp", lambda h: h.dma_start(out=sloc.rearrange("a p e -> p a e"), in_=self.S[:, :, :]), s9,
                           reads=self.Sbuf)
                self.out_toks.append(tk)
                s10 = P.new_dma_sem()
                tk = P.dma("sp", lambda h: h.dma_start(out=dloc[:, :], in_=self.DACC[:, :]), s10, reads=[self.daccbuf])
                self.out_toks.append(tk)

            if has_final:
                fst = Pool(self.FST)
                fst_sem = [P.new_dma_sem() for _ in range(16)]
                outTv = outT.rearrange("(c p) t -> p c t", p=128)
                for hf in range(2):
                    cs = slice(hf * TH, (hf + 1) * TH)
                    pst, psb = self.ps("ax")
                    for c in range(NCH):
                        sq, sqb = self.tb.get()
                        self.act(sq[:, :], self.X[:, c, cs], AF.Square, [self.XB[c][hf]], [sqb])
                        P.op("pe", lambda h, c=c, sq=sq, pst=pst: h.matmul(pst[:, :], lhsT=self.ones_d[:], rhs=sq[:, :],
                                                                          start=(c == 0), stop=(c == NCH - 1)),
                             reads=[sqb, self.cbuf], writes=[psb])
                    rs, rsb = self.tf.get()
                    self.act(rs[:, :], pst[:, :], AF.Ln, [psb], [rsb], bias=self.eps_ap[:, 0:1])
                    self.act(rs[:, :], rs[:, :], AF.Exp, [rsb], [rsb], scale=-0.5)
                    for c in range(NCH):
                        st, stb = fst.get()
                        si = (fst.i - 1) % 16
                        self.stt(st, self.X[:, c, cs], self.VEC[:, V_GFIN + c:V_GFIN + c + 1], rs[:, :], ALU.mult, ALU.mult,
                                 [self.XB[c][hf], rsb, self.vbuf], [stb])
                        tk = P.dma("sp", lambda h, c=c, cs=cs, st=st: h.dma_start(out=outTv[:, c, cs], in_=st),
                                   fst_sem[si], reads=[stb])
                        self.out_toks.append(tk)

            P.wait_all("sp", self.out_toks)
            P.emit()
        return nc


_PROGRAMS = {}


def get_program(mode):
    if mode not in _PROGRAMS:
        _PROGRAMS[mode] = K(mode).build()
    return _PROGRAMS[mode]


def _core_tokens(c):
    return c // 4, (c % 4) * T


def _rmask(c):
    j = c % 4
    m = np.zeros((128, 9), np.float32)
    for r in range(3):
        m[:, r] = 1.0 if r < j else 0.0
        m[:, 3 + r] = 0.0 if r < j else 1.0
        m[:, 6 + r] = 1.0 if r == j - 1 else 0.0
    return m


def kernel(**inputs):
    w = {k: np.asarray(v) for k, v in inputs.items()}
    x = w["x"]
    p = w["p"]
    const = make_const()
    cores = list(range(NCORES))
    mkeys = main_keys()
    was = [pack_tiles(w, l, mkeys) for l in range(DEPTH)]
    vec4 = np.stack([pack_vec(w, l) for l in range(DEPTH)], axis=0)
    wgk4 = np.stack([pack_wgk(w, l) for l in range(DEPTH)], axis=0)
    in_maps = []
    for c in cores:
        b, t0 = _core_tokens(c)
        m = {"xT": np.ascontiguousarray(x[b, t0:t0 + T, :].T), "const": const,
             "pT4": np.ascontiguousarray(p[:, b, t0:t0 + T, :].transpose(0, 2, 1)),
             "vec4": vec4, "wgk4": wgk4, "rmask": _rmask(c)}
        for l in range(DEPTH):
            m["wa%d" % l] = was[l]
        in_maps.append(m)
    nc = get_program("fused")
    res = run_bass_kernel_spmd(nc, in_maps, core_ids=cores)
    out = np.empty((2, 4096, D), np.float32)
    for c in cores:
        b, t0 = _core_tokens(c)
        out[b, t0:t0 + T, :] = res.results[c]["outT"].T
    return out
```
